# Optimizing a Trainium2 kernel written in Bass

```python
import jax, jax.numpy as jnp
from jax import lax
import numpy as np

D_MODEL = 2048
BATCH = 1
SEQ = 8192
DEPTH = 1
DEC_BATCH = 128
DEC_SEQ = 8
PAST_LEN = 2048
PAGE_SIZE = 128

ATT_WIDTH = D_MODEL // 2
CONV_WIDTH = D_MODEL - ATT_WIDTH
HEAD_DIM = 64
N_HEADS = ATT_WIDTH // HEAD_DIM
CONV_GROUPS = CONV_WIDTH // HEAD_DIM
WINDOWS = (128, 512, 2048)
DILATIONS = (1, 4, 16)
MAX_WINDOW = max(WINDOWS)
CONV_K = 3
D_FF = 4 * D_MODEL
Q_BLOCK = 128
EPS = 1e-6
IN_SIZES = (ATT_WIDTH, ATT_WIDTH, ATT_WIDTH, CONV_WIDTH, CONV_WIDTH, CONV_WIDTH)
IN_COLS = sum(IN_SIZES)

kernel_name = "hymba_dilated_swa_shortconv_decode_step"


def _key_distances():
    return np.stack([np.arange(w // d + 1, dtype=np.int32) * d for w, d in zip(WINDOWS, DILATIONS)])


def _alibi_slopes():
    return 2.0 ** (-8.0 * jnp.arange(1, N_HEADS + 1, dtype=jnp.float32) / N_HEADS)


def rmsnorm(x, g):
    xf = x.astype(jnp.float32)
    y = xf * lax.rsqrt(jnp.mean(xf * xf, axis=-1, keepdims=True) + EPS)
    return (y * g.astype(jnp.float32)).astype(x.dtype)


def _attend_block(q_blk, qi_blk, k_seq, v_seq):
    dist = jnp.asarray(_key_distances())
    idx = qi_blk[None, :, None] - dist[:, None, :]
    valid = idx >= 0
    idx_c = jnp.maximum(idx, 0)
    kg = jnp.take(k_seq, idx_c, axis=1)
    vg = jnp.take(v_seq, idx_c, axis=1)
    s = jnp.einsum('bqhd,bpqjhd->bpqhj', q_blk, kg).astype(jnp.float32)
    bias = -(_alibi_slopes()[None, None, :, None] * dist.astype(jnp.float32)[:, None, None, :])
    s = jnp.where(valid[None, :, :, None, :], s + bias, -jnp.inf)
    lse = jax.nn.logsumexp(s, axis=-1)
    p = jnp.exp(s - lse[..., None])
    o = jnp.einsum('bpqhj,bpqjhd->bpqhd', p, vg.astype(jnp.float32))
    alpha = jax.nn.softmax(lse, axis=1)
    out = jnp.einsum('bpqh,bpqhd->bqhd', alpha, o)
    return out.astype(q_blk.dtype)


def dilated_attention(q, k_seq, v_seq, q_start, block):
    B, S, H, Dh = q.shape
    nblk = S // block
    qb = q.reshape(B, nblk, block, H, Dh).transpose(1, 0, 2, 3, 4)
    qi = (q_start + jnp.arange(S, dtype=jnp.int32)).reshape(nblk, block)
    out = lax.map(lambda a: _attend_block(a[0], a[1], k_seq, v_seq), (qb, qi))
    return out.transpose(1, 0, 2, 3, 4).reshape(B, S, H, Dh)


def hybrid_layer(x, k_past, v_past, conv_past, block,
                 norm1_g, w_in, q_norm_g, k_norm_g, conv_w, w_out, norm2_g, w_up, w_down):
    B, S, _ = x.shape
    h = rmsnorm(x, norm1_g)
    proj = h @ w_in
    splits = [int(c) for c in np.cumsum(IN_SIZES)[:-1]]
    q, k, v, gb, gc, xc = jnp.split(proj, splits, axis=-1)
    q = rmsnorm(q.reshape(B, S, N_HEADS, HEAD_DIM), q_norm_g) * (HEAD_DIM ** -0.5)
    k = rmsnorm(k.reshape(B, S, N_HEADS, HEAD_DIM), k_norm_g)
    v = v.reshape(B, S, N_HEADS, HEAD_DIM)
    k_seq = jnp.concatenate([k_past, k], axis=1)
    v_seq = jnp.concatenate([v_past, v], axis=1)
    att = dilated_attention(q, k_seq, v_seq, k_past.shape[1], block).reshape(B, S, ATT_WIDTH)
    u_seq = jnp.concatenate([conv_past, gc * xc], axis=1)
    conv = u_seq[:, 0:S] * conv_w[0]
    for i in range(1, CONV_K):
        conv = conv + u_seq[:, i:i + S] * conv_w[i]
    cz = gb * conv
    x1 = x + jnp.concatenate([att, cz], axis=-1) @ w_out
    h2 = rmsnorm(x1, norm2_g)
    y = x1 + jnp.square(jax.nn.relu(h2 @ w_up)) @ w_down
    return y, k, v, u_seq[:, -(CONV_K - 1):]


def setup_inputs(seed: int = 0) -> dict:
    key = jax.random.key(seed)
    ks = jax.random.split(key, 16)
    f32 = jnp.float32
    buf = min(MAX_WINDOW, PAST_LEN)

    def nrm(k, shape, scale):
        return jax.random.normal(k, shape, f32) * scale

    return {
        "x_prompt": nrm(ks[0], (BATCH, SEQ, D_MODEL), 1.0),
        "x_sample": nrm(ks[1], (DEC_BATCH, DEC_SEQ, D_MODEL), 1.0),
        "state_k": nrm(ks[2], (DEPTH, DEC_BATCH, buf, N_HEADS, HEAD_DIM), 1.0),
        "state_v": nrm(ks[3], (DEPTH, DEC_BATCH, buf, N_HEADS, HEAD_DIM), 1.0),
        "state_conv": nrm(ks[4], (DEPTH, DEC_BATCH, CONV_K - 1, CONV_WIDTH), 1.0),
        "norm1_g": 1.0 + nrm(ks[5], (DEPTH, D_MODEL), 0.02),
        "w_in": nrm(ks[6], (DEPTH, D_MODEL, IN_COLS), D_MODEL ** -0.5),
        "q_norm_g": 1.0 + nrm(ks[7], (DEPTH, HEAD_DIM), 0.02),
        "k_norm_g": 1.0 + nrm(ks[8], (DEPTH, HEAD_DIM), 0.02),
        "conv_w": nrm(ks[9], (DEPTH, CONV_K, CONV_WIDTH), CONV_K ** -0.5),
        "w_out": nrm(ks[10], (DEPTH, D_MODEL, D_MODEL), D_MODEL ** -0.5),
        "norm2_g": 1.0 + nrm(ks[11], (DEPTH, D_MODEL), 0.02),
        "w_up": nrm(ks[12], (DEPTH, D_MODEL, D_FF), D_MODEL ** -0.5),
        "w_down": nrm(ks[13], (DEPTH, D_FF, D_MODEL), D_FF ** -0.5),
    }


def reference(x_prompt, x_sample, state_k, state_v, state_conv,
              norm1_g, w_in, q_norm_g, k_norm_g, conv_w, w_out, norm2_g, w_up, w_down):
    yp, ys = x_prompt, x_sample
    Bp, Sp, _ = x_prompt.shape
    keep_p = min(MAX_WINDOW, Sp)
    kp_l, vp_l, cp_l, ks_l, vs_l, cs_l = [], [], [], [], [], []
    for l in range(DEPTH):
        w = (norm1_g[l], w_in[l], q_norm_g[l], k_norm_g[l], conv_w[l], w_out[l],
             norm2_g[l], w_up[l], w_down[l])
        empty_kv = jnp.zeros((Bp, 0, N_HEADS, HEAD_DIM), yp.dtype)
        zero_conv = jnp.zeros((Bp, CONV_K - 1, CONV_WIDTH), yp.dtype)
        yp, kp, vp, cp = hybrid_layer(yp, empty_kv, empty_kv, zero_conv, Q_BLOCK, *w)
        ys, ksn, vsn, csn = hybrid_layer(ys, state_k[l], state_v[l], state_conv[l], 1, *w)
        kp_l.append(kp[:, -keep_p:])
        vp_l.append(vp[:, -keep_p:])
        cp_l.append(cp)
        ks_l.append(ksn)
        vs_l.append(vsn)
        cs_l.append(csn)
    new_k_prompt = jnp.stack(kp_l)
    new_v_prompt = jnp.stack(vp_l)
    new_conv_prompt = jnp.stack(cp_l)
    new_k_sample = jnp.stack(ks_l)
    new_v_sample = jnp.stack(vs_l)
    new_conv_sample = jnp.stack(cs_l)
    return (yp, ys, new_k_prompt, new_v_prompt, new_conv_prompt, new_k_sample, new_v_sample, new_conv_sample)
```

```python
import numpy as np
from contextlib import ExitStack
import concourse.bass as bass
import concourse.mybir as mybir
from concourse.bass_utils import run_bass_kernel_spmd

F32 = mybir.dt.float32
BF16 = mybir.dt.bfloat16
U8 = mybir.dt.uint8
ALU = mybir.AluOpType
AF = mybir.ActivationFunctionType
AX = mybir.AxisListType

NCORES = 8
D = 2048
DFF = 8192
NH = 16
HD = 64
AW = 1024
EPS = 1e-6
TP = 1024
TS = 128
TO = TP + TS
NHALO = 2048
NB = 16
SKIP_THRESH = 1e-30


class FW:
    ENG = ("pe", "act", "dve", "pool", "sp")

    def __init__(self, nc, stack):
        self.nc = nc
        self.stack = stack
        self.ops = {e: [] for e in self.ENG}
        self.cnt = {}
        self.sems = {}
        self.lastw = {}
        self.readers = {}
        self.iv = {}
        self.ovl = {}
        self.seen = {e: {} for e in self.ENG}
        self.stopped = False
        self.nrec = 0
        self.stop_n = None
        for e in self.ENG:
            self._sem("eng_" + e)

    def register(self, name, space, start, end):
        self.iv[name] = (space, start, end)
        self.ovl = {}

    def _over(self, name):
        o = self.ovl.get(name)
        if o is None:
            o = [name]
            if name in self.iv:
                sp, a, b = self.iv[name]
                for m, (sp2, a2, b2) in self.iv.items():
                    if m != name and sp2 == sp and a2 < b and a < b2:
                        o.append(m)
            self.ovl[name] = o
        return o

    def _sem(self, key):
        if key not in self.sems:
            self.sems[key] = self.stack.enter_context(self.nc.semaphore("s_" + key))
            self.cnt[key] = 0
        return key

    def _deps(self, eng, reads, writes, skip=None):
        ev = []
        for b in reads:
            for m in self._over(b):
                w = self.lastw.get(m)
                if w is not None:
                    ev.append(w)
        for b in writes:
            for m in self._over(b):
                w = self.lastw.get(m)
                if w is not None:
                    ev.append(w)
                ev.extend(self.readers.get(m, ()))
        waits = {}
        for (k, v) in ev:
            if (eng == "pe" and k == "eng_pe") or k == skip:
                continue
            if v > waits.get(k, 0):
                waits[k] = v
        out = []
        for k, v in waits.items():
            if self.seen[eng].get(k, 0) >= v:
                continue
            self.seen[eng][k] = v
            out.append((k, v))
        return out

    def _commit(self, event, reads, writes):
        for b in reads:
            self.readers.setdefault(b, []).append(event)
        for b in writes:
            self.lastw[b] = event
            self.readers[b] = []

    def op(self, eng, fn, reads=(), writes=()):
        self.nrec += 1
        if self.stop_n is not None and self.nrec > self.stop_n:
            self.stopped = True
        if self.stopped:
            return
        reads, writes = list(reads), list(writes)
        writes = writes + [r for r in reads if r.startswith("ps") and r not in writes]
        reads = [r for r in reads if not r.startswith("ps")]
        waits = self._deps(eng, reads, writes)
        k = "eng_" + eng
        self.cnt[k] += 1
        ev = (k, self.cnt[k])
        self.ops[eng].append((waits, fn, k, 1))
        self._commit(ev, reads, writes)

    def dma(self, eng, key, fn, reads=(), writes=()):
        self.nrec += 1
        if self.stop_n is not None and self.nrec > self.stop_n:
            self.stopped = True
        if self.stopped:
            return
        reads, writes = list(reads), list(writes)
        k = self._sem("d_" + key)
        waits = self._deps(eng, reads, writes, skip=k)
        self.cnt[k] += 16
        ev = (k, self.cnt[k])
        self.ops[eng].append((waits, fn, k, 16))
        self._commit(ev, reads, writes)

    def final_wait(self, eng, bufs):
        waits = self._deps(eng, list(bufs), ())
        self.ops[eng].append((waits, None, None, 0))

    def emit(self):
        nc = self.nc
        with nc.Block() as block:
            def mk(e):
                def body(engine):
                    for (waits, fn, k, inc) in self.ops[e]:
                        for (wk, wv) in waits:
                            engine.wait_ge(self.sems[wk], wv)
                        if fn is not None:
                            fn(engine).then_inc(self.sems[k], inc)
                return body
            block.tensor(mk("pe"))
            block.scalar(mk("act"))
            block.vector(mk("dve"))
            block.gpsimd(mk("pool"))
            block.sync(mk("sp"))


def _mult(d):
    d = np.asarray(d)
    m = ((d >= 0) & (d <= 128)).astype(np.float64)
    m += ((d >= 0) & (d <= 512) & (d % 4 == 0))
    m += ((d >= 0) & (d <= 2048) & (d % 16 == 0))
    return m


def _slopes():
    return 2.0 ** (-8.0 * np.arange(1, NH + 1, dtype=np.float64) / NH)


def _tables():
    sl = _slopes()
    b = np.arange(128)[:, None]
    a = np.arange(128)[None, :]
    Wp = np.zeros((8, 128, 17, 256), np.float32)
    for dl in range(17):
        d = 128 * dl + a - b
        m = _mult(d)
        for h in range(NH):
            w = m * np.exp(-sl[h] * np.maximum(d, 0))
            Wp[h // 2, :, dl, (h % 2) * 128:(h % 2) * 128 + 128] = w
    keep = [[dl for dl in range(17) if Wp[p, :, dl, :].max() > SKIP_THRESH] for p in range(8)]
    hh = np.repeat(np.arange(NH), 8)[None, :]
    tt = np.tile(np.arange(8), NH)[None, :]
    Wnear = np.zeros((128, 4, 128), np.float32)
    for kt in range(4):
        row = 1536 + 128 * kt + np.arange(128)[:, None]
        d = 2048 + tt - row
        Wnear[:, kt, :] = _mult(d) * np.exp(-sl[hh] * np.maximum(d, 0))
    Wfar = np.zeros((128, 8, 128), np.float32)
    for r in range(8):
        row = 16 * np.arange(96)[:, None] + r
        d = 2048 + tt - row
        w = _mult(d) * np.exp(-sl[hh] * np.maximum(d, 0))
        w = w * (tt == r)
        Wfar[:96, r, :] = w
    Wnew = np.zeros((128, NB, 128), np.float32)
    kb = (np.arange(128) // 8)[:, None]
    kt_ = (np.arange(128) % 8)[:, None]
    for bb in range(NB):
        d = tt - kt_
        Wnew[:, bb, :] = (kb == bb) * _mult(d) * np.exp(-sl[hh] * np.maximum(d, 0))
    return Wp, keep, Wnear, Wfar, Wnew


def build_program(keep, stop=None, small=False):
    nc = bass.Bass("TRN2", target_bir_lowering=False)
    stack = ExitStack()

    def din(name, shape, dt=F32):
        return nc.dram_tensor(name, list(shape), dt, kind="ExternalInput").ap()

    def dout(name, shape, dt=F32):
        return nc.dram_tensor(name, list(shape), dt, kind="ExternalOutput").ap()

    xh = din("xh", [NHALO, D])
    xo = din("xo", [TO, D])
    x2 = din("x2", [128, D])
    sk = din("sk", [NB, 2048, AW] if not small else [1, 16, AW])
    sv = din("sv", [NB, 2048, AW] if not small else [1, 16, AW])
    sc = din("sc", [2 * NB, AW])
    w_in = din("w_in", [D, 6144])
    w_out = din("w_out", [D, D])
    w_up = din("w_up", [D, DFF] if not small else [128, 128])
    w_down = din("w_down", [DFF, D] if not small else [128, 128])
    g1b = din("g1b", [128, D])
    g2b = din("g2b", [128, D])
    qgb = din("qgb", [128, AW])
    kgb = din("kgb", [128, AW])
    cwT = din("cwT", [128, 8, 3])
    valid = din("valid", [128, 24])
    identd = din("ident", [128, 128])
    Wp_d = din("Wp", [8, 128, 17, 256])
    Wnear_d = din("Wnear", [128, 4, 128])
    Wfar_d = din("Wfar", [128, 8, 128])
    Wnew_d = din("Wnew", [128, NB, 128])

    y_d = dout("y", [TO, D])
    ko_d = dout("ko", [TO, AW])
    vo_d = dout("vo", [TO, AW])
    uo_d = dout("uo", [34, AW])

    KB = 1024
    SB_BYTES = 207 * KB
    big = nc.alloc_sbuf_tensor("big", [128, SB_BYTES], U8)
    pbig = stack.enter_context(nc.psum_tensor("pbig", [128, 8 * 512], F32))
    fw = FW(nc, stack)
    if isinstance(stop, int):
        fw.stop_n = stop
    for bnk in range(8):
        fw.register("ps%d" % bnk, "psum", bnk * 2048, (bnk + 1) * 2048)

    class Arena:
        def __init__(self):
            self.off = 0
            self.uid = 0

        def seek(self, off):
            self.off = off

        def take(self, name, shape, dt):
            nb = int(np.prod(shape)) * mybir.dt.size(dt)
            nb_al = (nb + 63) // 64 * 64
            off = self.off
            self.off += nb_al
            assert off + nb_al <= SB_BYTES, ("SBUF overflow", name, off, nb_al)
            self.uid += 1
            uname = "%s#%d" % (name, self.uid)
            fw.register(uname, "sbuf", off, off + nb_al)
            v = big[:, off:off + nb].bitcast(dt)
            if len(shape) == 2:
                v = v.rearrange("p (a b) -> p a b", a=shape[0])
            elif len(shape) == 3:
                v = v.rearrange("p (a b c) -> p a b c", a=shape[0], b=shape[1])
            return v, uname

    def psum(bank, nbanks=1, dt=F32, shape=None):
        v = pbig[:, bank * 512:(bank + nbanks) * 512]
        if dt != F32:
            v = v.bitcast(dt)
        if shape is not None:
            v = v.rearrange("p (a b) -> p a b", a=shape[0])
        return v

    def wdma(key, wt, wn, src2d):
        for kc in range(16):
            fw.dma("pool", key, lambda e, kc=kc: e.dma_start(out=wt[:, kc, :], in_=src2d[kc * 128:(kc + 1) * 128, :]),
                   writes=[wn])

    def pnames(bank, n=1):
        return ["ps%d" % (bank + j) for j in range(n)]

    ar = Arena()
    R_C, R_A, R_B = 17 * KB, 53 * KB, 165 * KB
    ident, n_ident = ar.take("ident", [128], BF16)
    identf, n_identf = ar.take("identf", [128], F32)
    ones, n_ones = ar.take("ones", [128], BF16)
    valid_sb, n_valid = ar.take("valid", [24], F32)
    cw_sb, n_cw = ar.take("cw", [8, 3], F32)
    stat, n_stat = ar.take("stat", [16], F32)
    nst, n_nst = ar.take("nst", [16], F32)
    KTnew, n_KTnew = ar.take("KTnew", [8, 128], BF16)
    Vnew, n_Vnew = ar.take("Vnew", [AW], BF16)
    QTs, n_QTs = ar.take("QTs", [8, 128], BF16)
    assert ar.off <= 8 * KB, ar.off
    ar.seek(8 * KB)
    gS, n_gS = ar.take("gS", [9 * 256], F32)
    assert ar.off <= R_C
    ar.seek(R_C)
    catT, n_catT = ar.take("catT", [16, TO], BF16)
    g1 = gS[:, 0:D]

    fw.dma("pool", "c_ident", lambda e: e.dma_start(out=ident, in_=identd), writes=[n_ident])
    fw.dma("sp", "c_identf", lambda e: e.dma_start(out=identf, in_=identd), writes=[n_identf])
    fw.dma("sp", "c_valid", lambda e: e.dma_start(out=valid_sb, in_=valid), writes=[n_valid])
    fw.dma("sp", "c_cw", lambda e: e.dma_start(out=cw_sb, in_=cwT), writes=[n_cw])
    fw.dma("sp", "c_g1", lambda e: e.dma_start(out=g1, in_=g1b), writes=[n_gS])
    fw.op("dve", lambda e: e.memset(ones, 1.0), writes=[n_ones])

    def norm_transpose(src_ap, gb, n_gb, xt, n_xt, hb, n_hb, hT_dst, n_hT, slot, ps_bank):
        fw.dma("sp", "x%d" % slot, lambda e: e.dma_start(out=xt, in_=src_ap), writes=[n_xt])
        st = stat[:, 4 * slot:4 * slot + 4]
        fw.op("act", lambda e: e.activation(out=hb, in_=xt, func=AF.Square, accum_out=st[:, 0:1]),
              reads=[n_xt], writes=[n_hb, n_stat])
        fw.op("dve", lambda e: e.tensor_scalar(st[:, 1:2], st[:, 0:1], 1.0 / D, EPS, ALU.mult, ALU.add),
              reads=[n_stat], writes=[n_stat])
        fw.op("dve", lambda e: e.reciprocal(st[:, 2:3], st[:, 1:2]), reads=[n_stat], writes=[n_stat])
        fw.op("act", lambda e: e.activation(out=st[:, 3:4], in_=st[:, 2:3], func=AF.Sqrt),
              reads=[n_stat], writes=[n_stat])
        fw.op("dve", lambda e: e.scalar_tensor_tensor(out=hb, in0=xt, scalar=st[:, 3:4], in1=gb,
                                                      op0=ALU.mult, op1=ALU.mult),
              reads=[n_xt, n_stat, n_gb], writes=[n_hb])
        pT = psum(ps_bank, 2, BF16, [16, 128])
        pn = pnames(ps_bank, 2)
        for kc in range(16):
            fw.op("pe", lambda e, kc=kc: e.transpose(pT[:, kc, :], hb[:, kc * 128:(kc + 1) * 128], ident),
                  reads=[n_hb, n_ident], writes=pn)
        fw.op("act", lambda e: e.copy(hT_dst[:, 0:8, :], pT[:, 0:8, :]), reads=[pn[0]], writes=[n_hT])
        fw.op("dve", lambda e: e.tensor_copy(hT_dst[:, 8:16, :], pT[:, 8:16, :]), reads=[pn[1]], writes=[n_hT])

    ar.seek(R_A)
    KT, n_KT = ar.take("KT", [8, 3072], BF16)
    Vb, n_Vb = ar.take("Vb", [24, AW], BF16)
    QT, n_QT = ar.take("QT", [8, TP], BF16)
    assert ar.off <= R_B, ar.off
    ar.seek(R_C)
    wblk = [ar.take("wblk%d" % i, [16, 512], BF16) for i in range(2)]
    hg, n_hg = ar.take("hg", [AW], F32)
    assert ar.off <= R_A
    ar.seek(R_B)
    xts = [ar.take("xt%d" % i, [D], F32) for i in range(2)]
    hbs = [ar.take("hb%d" % i, [D], BF16) for i in range(2)]
    hTs = [ar.take("hT%d" % i, [16, 128], BF16) for i in range(2)]
    tmpf, n_tmpf = ar.take("tmpf", [AW], F32)
    kst, n_kst = ar.take("kst", [AW], F32)
    knb, n_knb = ar.take("knb", [AW], BF16)
    assert ar.off <= SB_BYTES

    def head_norm(ps_ap, psn, out_f32, out_bf, scale):
        ns = nst[:, 0:16]
        for hf in range(2):
            fw.op("act", lambda e, hf=hf: e.activation(out=tmpf[:, hf * 512:(hf + 1) * 512],
                                                       in_=ps_ap[:, hf * 512:(hf + 1) * 512], func=AF.Square),
                  reads=[psn[hf]], writes=[n_tmpf])
        fw.op("dve", lambda e: e.tensor_reduce(out=ns, in_=tmpf.rearrange("p (h d) -> p h d", d=HD),
                                               axis=AX.X, op=ALU.add), reads=[n_tmpf], writes=[n_nst])
        fw.op("dve", lambda e: e.tensor_scalar(ns, ns, 1.0 / HD, EPS, ALU.mult, ALU.add), reads=[n_nst], writes=[n_nst])
        fw.op("dve", lambda e: e.reciprocal(ns, ns), reads=[n_nst], writes=[n_nst])
        fw.op("act", lambda e: e.activation(out=ns, in_=ns, func=AF.Sqrt), reads=[n_nst], writes=[n_nst])
        for hf in range(2):
            fw.op("dve", lambda e, hf=hf: e.tensor_tensor(
                out=tmpf[:, hf * 512:(hf + 1) * 512].rearrange("p (h d) -> p h d", d=HD),
                in0=ps_ap[:, hf * 512:(hf + 1) * 512].rearrange("p (h d) -> p h d", d=HD),
                in1=ns[:, hf * 8:(hf + 1) * 8].unsqueeze(2).to_broadcast([128, 8, HD]), op=ALU.mult),
                reads=[psn[hf], n_nst], writes=[n_tmpf])
        fw.op("dve", lambda e: e.tensor_tensor(out=out_f32, in0=tmpf, in1=hg, op=ALU.mult),
              reads=[n_tmpf, n_hg], writes=[n_kst])
        fw.op("act", lambda e: e.activation(out=out_bf, in_=out_f32, func=AF.Copy, scale=scale),
              reads=[n_kst], writes=[n_knb])

    tcount = [0]

    def transpose8(dst_ap, n_dst):
        bank = 6 + (tcount[0] % 2)
        tcount[0] += 1
        pT = psum(bank, 1, BF16, [8, 128])
        pn = pnames(bank)
        for p in range(8):
            fw.op("pe", lambda e, p=p: e.transpose(pT[:, p, :], knb[:, p * 128:(p + 1) * 128], ident),
                  reads=[n_knb, n_ident], writes=pn)
        fw.op("act", lambda e: e.copy(dst_ap, pT), reads=pn, writes=[n_dst])

    for (pname, wcol, tiles) in (("k", 1024, range(25)), ("v", 2048, range(25)), ("q", 0, range(16, 25))):
        for i in range(2):
            wdma("wblk%d" % i, wblk[i][0], wblk[i][1], w_in[:, wcol + i * 512:wcol + (i + 1) * 512])
        if pname == "k":
            fw.dma("sp", "hg", lambda e: e.dma_start(out=hg, in_=kgb), writes=[n_hg])
        elif pname == "q":
            fw.dma("sp", "hg", lambda e: e.dma_start(out=hg, in_=qgb), writes=[n_hg])
        for ti in tiles:
            slot = ti % 2
            src = xh[ti * 128:(ti + 1) * 128, :] if ti < 16 else xo[(ti - 16) * 128:(ti - 15) * 128, :]
            own = ti >= 16
            r0 = (ti - 16) * 128
            hT, n_hT = hTs[slot]
            norm_transpose(src, g1, n_gS, xts[slot][0], xts[slot][1], hbs[slot][0], hbs[slot][1],
                           hT, n_hT, slot, 2 * slot)
            pk = psum(4, 2)
            pkn = pnames(4, 2)
            for nch in range(2):
                for kc in range(16):
                    fw.op("pe", lambda e, nch=nch, kc=kc, hT=hT: e.matmul(
                        pk[:, nch * 512:(nch + 1) * 512], hT[:, kc, :], wblk[nch][0][:, kc, :],
                        start=(kc == 0), stop=(kc == 15)),
                        reads=[n_hT, wblk[nch][1]], writes=[pkn[nch]])
            if pname == "v":
                vdst, n_vdst = (Vb[:, ti, :], n_Vb) if ti < 24 else (Vnew, n_Vnew)
                for hf in range(2):
                    fw.op("dve", lambda e, vdst=vdst, hf=hf: e.tensor_copy(
                        vdst[:, hf * 512:(hf + 1) * 512], pk[:, hf * 512:(hf + 1) * 512]),
                        reads=[pkn[hf]], writes=[n_vdst])
                if own:
                    for hf in range(2):
                        fw.op("act", lambda e, hf=hf: e.copy(kst[:, hf * 512:(hf + 1) * 512],
                                                             pk[:, hf * 512:(hf + 1) * 512]),
                              reads=[pkn[hf]], writes=[n_kst])
                    fw.dma("sp", "vo", lambda e, r0=r0: e.dma_start(out=vo_d[r0:r0 + 128, :], in_=kst),
                           reads=[n_kst], writes=["vo_out"])
            elif pname == "k":
                head_norm(pk, pkn, kst, knb, 1.0)
                if own:
                    fw.dma("sp", "ko", lambda e, r0=r0: e.dma_start(out=ko_d[r0:r0 + 128, :], in_=kst),
                           reads=[n_kst], writes=["ko_out"])
                if ti < 24:
                    transpose8(KT[:, :, ti * 128:(ti + 1) * 128], n_KT)
                else:
                    transpose8(KTnew, n_KTnew)
            else:
                head_norm(pk, pkn, kst, knb, HD ** -0.5)
                if ti < 24:
                    transpose8(QT[:, :, r0:r0 + 128], n_QT)
                else:
                    transpose8(QTs, n_QTs)

    if stop == "kv":
        fw.stopped = True
    ar.seek(R_B)
    Wps = [ar.take("Wp%d" % i, [17, 256], BF16) for i in range(2)]
    Es = [ar.take("E%d" % i, [256], F32) for i in range(3)]
    Ps = [ar.take("P%d" % i, [256], BF16) for i in range(3)]
    rden, n_rden = ar.take("rden", [256], F32)
    vones, n_vones = ar.take("vones", [16, 128], BF16)
    fw.op("dve", lambda e: e.tensor_copy(vones, valid_sb[:, 0:16].unsqueeze(2).to_broadcast([128, 16, 128])),
          reads=[n_valid], writes=[n_vones])
    QTbd = [ar.take("QTbd%d" % i, [8, 256], BF16) for i in range(2)]
    for i in range(2):
        fw.op("dve", lambda e, i=i: e.memset(QTbd[i][0], 0.0), writes=[QTbd[i][1]])
    it = 0
    for p in range(8):
        Qb, Qbn = QTbd[p % 2]
        for hh in range(2):
            fw.op("dve", lambda e, hh=hh, p=p, Qb=Qb: e.tensor_copy(
                Qb[hh * 64:(hh + 1) * 64, :, hh * 128:(hh + 1) * 128],
                QT[hh * 64:(hh + 1) * 64, p, :].rearrange("p (q a) -> p q a", a=128)),
                reads=[n_QT], writes=[Qbn])
        Wt, Wn = Wps[p % 2]
        fw.dma("pool", "Wp%d" % (p % 2), lambda e, p=p, Wt=Wt: e.dma_start(out=Wt, in_=Wp_d[p]), writes=[Wn])
        dls = keep[p]
        for qb in range(8):
            accb = qb % 2
            acc = psum(accb, 1)
            accd = psum(5 + accb, 1)
            accn = pnames(accb)
            accdn = pnames(5 + accb)
            qcol = qb * 128
            for i, dl in enumerate(dls):
                kt = 16 + qb - dl
                kcol = kt * 128
                sbank = 2 + (it % 3)
                S = psum(sbank, 1)[:, 0:256]
                Sn = pnames(sbank)
                E, En = Es[it % 3]
                P, Pn = Ps[it % 3]
                it += 1
                fw.op("pe", lambda e, S=S, kcol=kcol, qb=qb, p=p, Qb=Qb: e.matmul(
                    S, KT[:, p, kcol:kcol + 128], Qb[:, qb, :], start=True, stop=True),
                    reads=[n_KT, Qbn], writes=Sn)
                fw.op("act", lambda e, E=E, S=S: e.activation(out=E, in_=S, func=AF.Exp), reads=Sn, writes=[En])
                fw.op("dve", lambda e, P=P, E=E, dl=dl, Wt=Wt: e.tensor_tensor(
                    out=P, in0=E, in1=Wt[:, dl, :], op=ALU.mult), reads=[En, Wn], writes=[Pn])
                first = (i == 0)
                last = (i == len(dls) - 1)
                fw.op("pe", lambda e, P=P, kt=kt, p=p, acc=acc, first=first, last=last: e.matmul(
                    acc[:, 0:256], Vb[:, kt, p * 128:(p + 1) * 128], P, start=first, stop=last),
                    reads=[Pn, n_Vb], writes=accn)
                onesv = vones[:, kt, :] if kt < 16 else ones
                fw.op("pe", lambda e, P=P, accd=accd, first=first, last=last, onesv=onesv: e.matmul(
                    accd[:, 0:256], onesv, P, start=first, stop=last),
                    reads=[Pn, n_ones, n_vones], writes=accdn)
            fw.op("dve", lambda e, accd=accd: e.reciprocal(rden, accd[:, 0:256]), reads=accdn, writes=[n_rden])
            fw.op("dve", lambda e, acc=acc, p=p, qcol=qcol: e.tensor_tensor(
                out=catT[0:64, p, qcol:qcol + 128], in0=acc[0:64, 0:128], in1=rden[0:64, 0:128], op=ALU.mult),
                reads=accn + [n_rden], writes=[n_catT])
            fw.op("dve", lambda e, acc=acc, p=p, qcol=qcol: e.tensor_tensor(
                out=catT[64:128, p, qcol:qcol + 128], in0=acc[64:128, 128:256], in1=rden[64:128, 128:256],
                op=ALU.mult), reads=accn + [n_rden], writes=[n_catT])

    if stop == "attp":
        fw.stopped = True
    ar.seek(R_A)
    Qbd, n_Qbd = ar.take("Qbd", [8, NB, 16], BF16)
    Wnear, n_Wnear = ar.take("Wnear", [4, 128], BF16)
    Wfar, n_Wfar = ar.take("Wfar", [8, 128], BF16)
    Wnew, n_Wnew = ar.take("Wnew", [NB, 128], BF16)
    kfar = [ar.take("kfar%d" % i, [8, AW], BF16) for i in range(2)]
    knear = [ar.take("knear%d" % i, [4, AW], BF16) for i in range(2)]
    vfar = [ar.take("vfar%d" % i, [8, AW], BF16) for i in range(2)]
    vnear = [ar.take("vnear%d" % i, [4, AW], BF16) for i in range(2)]
    KTs = [ar.take("KTs%d" % i, [8, 128], BF16) for i in range(3)]
    Es2 = [ar.take("E2_%d" % i, [128], F32) for i in range(3)]
    Pall, n_Pall = ar.take("Pall", [13, 128], BF16)
    rden2, n_rden2 = ar.take("rden2", [128], F32)
    osb, n_osb = ar.take("osb", [128], F32)

    fw.dma("pool", "Wnear", lambda e: e.dma_start(out=Wnear, in_=Wnear_d), writes=[n_Wnear])
    fw.dma("pool", "Wfar", lambda e: e.dma_start(out=Wfar, in_=Wfar_d), writes=[n_Wfar])
    fw.dma("pool", "Wnew", lambda e: e.dma_start(out=Wnew, in_=Wnew_d), writes=[n_Wnew])
    fw.op("dve", lambda e: e.memset(Qbd, 0.0), writes=[n_Qbd])
    for hh in range(2):
        fw.op("dve", lambda e, hh=hh: e.tensor_copy(
            Qbd[hh * 64:(hh + 1) * 64, :, :, hh * 8:(hh + 1) * 8],
            QTs[hh * 64:(hh + 1) * 64, :, :].rearrange("p a (b t) -> p a b t", t=8)),
            reads=[n_QTs], writes=[n_Qbd])

    it = 0
    for b in range(NB if _DBG_NB is None else _DBG_NB):
        s = b % 2
        for (far_t, near_t, srcd, nm) in ((kfar[s], knear[s], sk, "k"), (vfar[s], vnear[s], sv, "v")):
            for r in range(8):
                fw.dma("pool", "%sfar%d" % (nm, s), lambda e, b=b, r=r, far_t=far_t, srcd=srcd: e.dma_start(
                    out=far_t[0][0:96, r, :],
                    in_=srcd[b, 0:1536, :].rearrange("(j r) c -> j r c", r=16)[:, r, :]), writes=[far_t[1]])
            for t in range(4):
                fw.dma("pool", "%snear%d" % (nm, s), lambda e, b=b, t=t, near_t=near_t, srcd=srcd: e.dma_start(
                    out=near_t[0][:, t, :], in_=srcd[b, 1536 + t * 128:1536 + (t + 1) * 128, :]), writes=[near_t[1]])
        accb = b % 2
        accO = psum(accb, 1)[:, 0:128]
        accD = psum(7, 1)[:, 0:128]
        accn = pnames(accb)
        accdn2 = pnames(7)
        vlist = []
        tiles = [("far", r) for r in range(8)] + [("near", k) for k in range(4)] + [("new", 0)]
        for i, (kind, j) in enumerate(tiles):
            nk = 96 if kind == "far" else 128
            if kind == "far":
                ksrc, ksn = kfar[s][0][0:96, j, :], kfar[s][1]
                vsrc, vsn = vfar[s][0][0:96, j, :], vfar[s][1]
                W, Wn = Wfar[0:96, j, :], n_Wfar
            elif kind == "near":
                ksrc, ksn = knear[s][0][:, j, :], knear[s][1]
                vsrc, vsn = vnear[s][0][:, j, :], vnear[s][1]
                W, Wn = Wnear[:, j, :], n_Wnear
            else:
                ksrc, ksn = None, None
                vsrc, vsn = Vnew, n_Vnew
                W, Wn = Wnew[:, b, :], n_Wnew
            r3 = it % 3
            it += 1
            if kind != "new":
                KTt, KTn = KTs[r3]
                tb = 5 + (it % 2)
                pT = psum(tb, 1, BF16, [8, 128])
                for p in range(8):
                    fw.op("pe", lambda e, p=p, pT=pT, ksrc=ksrc, nk=nk: e.transpose(
                        pT[:, p, 0:nk], ksrc[:, p * 128:(p + 1) * 128], ident[0:nk, 0:nk]),
                        reads=[ksn, n_ident], writes=pnames(tb))
                fw.op("dve", lambda e, KTt=KTt, pT=pT, nk=nk: e.tensor_copy(KTt[:, :, 0:nk], pT[:, :, 0:nk]),
                      reads=pnames(tb), writes=[KTn])
                ktv = lambda p, KTt=KTt, nk=nk: KTt[:, p, 0:nk]
            else:
                KTn = n_KTnew
                ktv = lambda p: KTnew[:, p, :]
            sbank = 2 + r3
            S = psum(sbank, 1)[0:nk, 0:128]
            Sn = pnames(sbank)
            for p in range(8):
                fw.op("pe", lambda e, p=p, S=S, ktv=ktv, b=b: e.matmul(
                    S[:, p * 16:(p + 1) * 16], ktv(p), Qbd[:, p, b, :], start=True, stop=True),
                    reads=[KTn, n_Qbd], writes=Sn)
            E, En = Es2[r3][0][0:nk], Es2[r3][1]
            P = Pall[0:nk, i, :]
            fw.op("act", lambda e, E=E, S=S: e.activation(out=E, in_=S, func=AF.Exp), reads=Sn, writes=[En])
            fw.op("dve", lambda e, P=P, E=E, W=W: e.tensor_tensor(out=P, in0=E, in1=W, op=ALU.mult),
                  reads=[En, Wn], writes=[n_Pall])
            vlist.append((vsrc, vsn, nk))
        nt = len(tiles)
        for p in range(8):
            for i, (vsrc, vsn, nk) in enumerate(vlist):
                fw.op("pe", lambda e, p=p, i=i, vsrc=vsrc, nk=nk, accO=accO: e.matmul(
                    accO[:, p * 16:(p + 1) * 16], vsrc[:, p * 128:(p + 1) * 128], Pall[0:nk, i, p * 16:(p + 1) * 16],
                    start=(i == 0), stop=(i == nt - 1)), reads=[n_Pall, vsn], writes=accn)
        for i, (vsrc, vsn, nk) in enumerate(vlist):
            fw.op("pe", lambda e, i=i, nk=nk, accD=accD, nt=nt: e.matmul(
                accD, ones[0:nk, :], Pall[0:nk, i, :], start=(i == 0), stop=(i == nt - 1)),
                reads=[n_Pall, n_ones], writes=accdn2)
        fw.op("dve", lambda e, accD=accD: e.reciprocal(rden2, accD), reads=accdn2, writes=[n_rden2])
        fw.op("dve", lambda e, accO=accO: e.tensor_copy(osb, accO), reads=accn, writes=[n_osb])
        for hh in range(2):
            fw.op("dve", lambda e, hh=hh, b=b: e.tensor_tensor(
                out=catT[hh * 64:(hh + 1) * 64, 0:8, TP + b * 8:TP + b * 8 + 8],
                in0=osb[hh * 64:(hh + 1) * 64, :].rearrange("p (a c) -> p a c", c=16)[:, :, hh * 8:(hh + 1) * 8],
                in1=rden2[hh * 64:(hh + 1) * 64, :].rearrange("p (a c) -> p a c", c=16)[:, :, hh * 8:(hh + 1) * 8],
                op=ALU.mult), reads=[n_osb, n_rden2], writes=[n_catT])

    if stop == "atts":
        fw.stopped = True
    ar.seek(R_A)
    NCV = 2 + TP + TS
    hTc, n_hTc = ar.take("hTc", [16, NCV], BF16)
    wbc = [ar.take("wbc%d" % i, [16, 512], BF16) for i in range(3)]
    xtc, n_xtc = ar.take("xtc", [D], F32)
    hbc, n_hbc = ar.take("hbc", [D], BF16)
    hTt = [ar.take("hTt%d" % i, [16, 128], BF16) for i in range(2)]
    uext_p, n_uext_p = ar.take("uext_p", [TP + 2], F32)
    uext_s, n_uext_s = ar.take("uext_s", [NB, 10], F32)
    xcs, n_xcs = ar.take("xcs", [NCV], F32)
    acc_c, n_acc_c = ar.take("acc_c", [TO], F32)
    scsb, n_scsb = ar.take("scsb", [AW], F32)
    usel, n_usel = ar.take("usel", [34], F32)
    uosb, n_uosb = ar.take("uosb", [AW], F32)

    fw.dma("sp", "scsb", lambda e: e.dma_start(out=scsb[0:32], in_=sc), writes=[n_scsb])
    for ti in range(10):
        slot = ti % 2
        src = x2 if ti == 0 else xo[(ti - 1) * 128:ti * 128, :]
        norm_transpose(src, g1, n_gS, xtc, n_xtc, hbc, n_hbc, hTt[slot][0], hTt[slot][1], 2, 2 * slot)
        if ti == 0:
            fw.op("dve", lambda e, slot=slot: e.tensor_copy(hTc[:, :, 0:2], hTt[slot][0][:, :, 0:2]),
                  reads=[hTt[slot][1]], writes=[n_hTc])
        else:
            c0 = 2 + (ti - 1) * 128
            fw.op("pool", lambda e, slot=slot, c0=c0: e.tensor_copy(hTc[:, :, c0:c0 + 128], hTt[slot][0]),
                  reads=[hTt[slot][1]], writes=[n_hTc])

    chunks = [(0, 512), (512, 512), (1024, NCV - 1024)]
    pbank = 0
    for c in range(8):
        if c % 4 == 0:
            half = c // 4
            for i, cb in enumerate((2048, 1024, 0)):
                wdma("wbc%d" % i, wbc[i][0], wbc[i][1], w_in[:, 3072 + cb + half * 512:3072 + cb + (half + 1) * 512])
        cc = c % 4
        res = {}
        for wi, nm in enumerate(("xc", "C", "B")):
            wt, wn = wbc[wi]
            for (t0, tn) in chunks:
                bank = pbank % 8
                pbank += 1
                pp = psum(bank, 1)[:, 0:tn]
                pn = pnames(bank)
                for kc in range(16):
                    fw.op("pe", lambda e, kc=kc, pp=pp, t0=t0, tn=tn, wt=wt, cc=cc: e.matmul(
                        pp, wt[:, kc, cc * 128:(cc + 1) * 128], hTc[:, kc, t0:t0 + tn],
                        start=(kc == 0), stop=(kc == 15)), reads=[n_hTc, wn], writes=pn)
                res[(nm, t0)] = (pp, pn)
                if nm == "xc":
                    fw.op("act", lambda e, pp=pp, t0=t0, tn=tn: e.copy(xcs[:, t0:t0 + tn], pp),
                          reads=pn, writes=[n_xcs])
                elif nm == "C":
                    if t0 < 1024:
                        fw.op("dve", lambda e, pp=pp, t0=t0, tn=tn: e.tensor_tensor(
                            out=uext_p[:, t0:t0 + tn], in0=pp, in1=xcs[:, t0:t0 + tn], op=ALU.mult),
                            reads=pn + [n_xcs], writes=[n_uext_p])
                    else:
                        fw.op("dve", lambda e, pp=pp: e.tensor_tensor(
                            out=uext_p[:, 1024:1026], in0=pp[:, 0:2], in1=xcs[:, 1024:1026], op=ALU.mult),
                            reads=pn + [n_xcs], writes=[n_uext_p])
                        fw.op("dve", lambda e, pp=pp: e.tensor_tensor(
                            out=uext_s[:, :, 2:10], in0=pp[:, 2:130].rearrange("p (b t) -> p b t", t=8),
                            in1=xcs[:, 1026:1154].rearrange("p (b t) -> p b t", t=8), op=ALU.mult),
                            reads=pn + [n_xcs], writes=[n_uext_s])
        tb = pbank % 8
        pbank += 1
        pst = psum(tb, 1)[:, 0:32]
        fw.op("pe", lambda e, pst=pst, c=c: e.transpose(pst, scsb[0:32, c * 128:(c + 1) * 128], identf[0:32, 0:32]),
              reads=[n_scsb, n_identf], writes=pnames(tb))
        fw.op("act", lambda e, pst=pst: e.copy(uext_s[:, :, 0:2], pst.rearrange("p (b j) -> p b j", j=2)),
              reads=pnames(tb), writes=[n_uext_s])
        for (dst, srcf, un) in ((acc_c[:, 0:TP], lambda k: uext_p[:, k:k + TP], n_uext_p),
                                (acc_c[:, TP:TO].rearrange("p (b t) -> p b t", t=8),
                                 lambda k: uext_s[:, :, k:k + 8], n_uext_s)):
            fw.op("dve", lambda e, dst=dst, srcf=srcf, c=c: e.tensor_scalar(
                dst, srcf(0), cw_sb[:, c, 0:1], None, ALU.mult), reads=[un, n_cw], writes=[n_acc_c])
            for k in (1, 2):
                fw.op("dve", lambda e, dst=dst, srcf=srcf, c=c, k=k: e.scalar_tensor_tensor(
                    out=dst, in0=srcf(k), scalar=cw_sb[:, c, k:k + 1], in1=dst, op0=ALU.mult, op1=ALU.add),
                    reads=[un, n_cw, n_acc_c], writes=[n_acc_c])
        for (t0, tn) in chunks:
            pp, pn = res[("B", t0)]
            if t0 == 0:
                fw.op("dve", lambda e, pp=pp, c=c: e.tensor_tensor(
                    out=catT[:, 8 + c, 0:510], in0=pp[:, 2:512], in1=acc_c[:, 0:510], op=ALU.mult),
                    reads=pn + [n_acc_c], writes=[n_catT])
            elif t0 == 512:
                fw.op("dve", lambda e, pp=pp, c=c: e.tensor_tensor(
                    out=catT[:, 8 + c, 510:1022], in0=pp[:, 0:512], in1=acc_c[:, 510:1022], op=ALU.mult),
                    reads=pn + [n_acc_c], writes=[n_catT])
            else:
                fw.op("dve", lambda e, pp=pp, c=c: e.tensor_tensor(
                    out=catT[:, 8 + c, 1022:1152], in0=pp[:, 0:130], in1=acc_c[:, 1022:1152], op=ALU.mult),
                    reads=pn + [n_acc_c], writes=[n_catT])
        fw.op("act", lambda e: e.copy(usel[:, 0:2], uext_p[:, 1024:1026]), reads=[n_uext_p], writes=[n_usel])
        fw.op("act", lambda e: e.copy(usel[:, 2:34].rearrange("p (b j) -> p b j", j=2), uext_s[:, :, 8:10]),
              reads=[n_uext_s], writes=[n_usel])
        tb = pbank % 8
        pbank += 1
        pu = psum(tb, 1)[0:34, 0:128]
        fw.op("pe", lambda e, pu=pu: e.transpose(pu, usel, identf), reads=[n_usel, n_identf], writes=pnames(tb))
        fw.op("act", lambda e, pu=pu, c=c: e.copy(uosb[0:34, c * 128:(c + 1) * 128], pu),
              reads=pnames(tb), writes=[n_uosb])
    fw.dma("sp", "uo", lambda e: e.dma_start(out=uo_d, in_=uosb[0:34]), reads=[n_uosb], writes=["uo_out"])

    if stop == "conv":
        fw.stopped = True
    ar.seek(R_A)
    yacc, n_yacc0 = ar.take("yacc", [9, D], F32)
    h2T, n_h2T = ar.take("h2T", [16, TO], BF16)
    wob = [ar.take("wob%d" % i, [16, 512], BF16) for i in range(2)]
    xrc = [ar.take("xrc%d" % i, [512], F32) for i in range(2)]
    h2b, n_h2b = ar.take("h2b", [D], BF16)
    R_F = ar.off
    g2 = gS[:, 0:D]
    fw.dma("sp", "c_g2", lambda e: e.dma_start(out=g2, in_=g2b), writes=[n_gS])
    xi = 0
    for nch in range(4):
        wt, wn = wob[nch % 2]
        wdma("wob%d" % (nch % 2), wt, wn, w_out[:, nch * 512:(nch + 1) * 512])
        for ti in range(9):
            xr_, xn = xrc[xi % 2]
            fw.dma("sp", "xrc%d" % (xi % 2), lambda e, ti=ti, nch=nch, xr_=xr_: e.dma_start(
                out=xr_, in_=xo[ti * 128:(ti + 1) * 128, nch * 512:(nch + 1) * 512]), writes=[xn])
            bank = xi % 8
            xi += 1
            po = psum(bank, 1)
            pn = pnames(bank)
            for kc in range(16):
                fw.op("pe", lambda e, kc=kc, po=po, ti=ti, wt=wt: e.matmul(
                    po, catT[:, kc, ti * 128:(ti + 1) * 128], wt[:, kc, :], start=(kc == 0), stop=(kc == 15)),
                    reads=[n_catT, wn], writes=pn)
            fw.op("dve", lambda e, po=po, ti=ti, nch=nch, xr_=xr_: e.tensor_tensor(
                out=yacc[:, ti, nch * 512:(nch + 1) * 512], in0=po, in1=xr_, op=ALU.add),
                reads=pn + [xn], writes=[n_yacc0])
    for ti in range(9):
        ya = yacc[:, ti, :]
        st = stat[:, 12:16]
        fw.op("act", lambda e, ya=ya: e.activation(out=h2b, in_=ya, func=AF.Square, accum_out=st[:, 0:1]),
              reads=[n_yacc0], writes=[n_h2b, n_stat])
        fw.op("dve", lambda e: e.tensor_scalar(st[:, 1:2], st[:, 0:1], 1.0 / D, EPS, ALU.mult, ALU.add),
              reads=[n_stat], writes=[n_stat])
        fw.op("dve", lambda e: e.reciprocal(st[:, 2:3], st[:, 1:2]), reads=[n_stat], writes=[n_stat])
        fw.op("act", lambda e: e.activation(out=st[:, 3:4], in_=st[:, 2:3], func=AF.Sqrt),
              reads=[n_stat], writes=[n_stat])
        fw.op("dve", lambda e, ya=ya: e.scalar_tensor_tensor(out=h2b, in0=ya, scalar=st[:, 3:4], in1=g2,
                                                             op0=ALU.mult, op1=ALU.mult),
              reads=[n_yacc0, n_stat, n_gS], writes=[n_h2b])
        tb0 = 2 * (ti % 2)
        pT = psum(tb0, 2, BF16, [16, 128])
        pn = pnames(tb0, 2)
        for kc in range(16):
            fw.op("pe", lambda e, kc=kc, pT=pT: e.transpose(pT[:, kc, :], h2b[:, kc * 128:(kc + 1) * 128], ident),
                  reads=[n_h2b, n_ident], writes=pn)
        fw.op("act", lambda e, pT=pT, ti=ti: e.copy(h2T[:, 0:8, ti * 128:(ti + 1) * 128], pT[:, 0:8, :]),
              reads=[pn[0]], writes=[n_h2T])
        fw.op("dve", lambda e, pT=pT, ti=ti: e.tensor_copy(h2T[:, 8:16, ti * 128:(ti + 1) * 128], pT[:, 8:16, :]),
              reads=[pn[1]], writes=[n_h2T])

    if stop == "out":
        fw.stopped = True
    ar.seek(R_C)
    wus = [ar.take("wu%d" % i, [16, 512], BF16) for i in range(2)]
    rl = [ar.take("rl%d" % i, [512], F32) for i in range(2)]
    assert ar.off <= R_A, ar.off
    ar.seek(R_A + 108 * KB)
    wds = [ar.take("wd%d" % i, [4, D], BF16) for i in range(2)]
    hid0 = ar.take("hid0", [4, TO], BF16)
    hid1 = (gS[:, :].bitcast(BF16)[:, 0:4 * TO].rearrange("p (a b) -> p a b", a=4), n_gS)
    hid = [hid0, hid1]
    NFB = DFF // 512
    tchunks = [(0, 512), (512, 512), (1024, 128)]
    pb = 0
    ri = 0
    for fb in range(NFB):
        s = fb % 2
        wu, wun = wus[s]
        wd, wdn = wds[s]
        hd, hn = hid[s]
        if not fw.stopped:
            wdma("wu%d" % s, wu, wun, w_up[:, fb * 512:(fb + 1) * 512])
        for fc in range(4):
            fw.dma("pool", "wd%d" % s, lambda e, fb=fb, wd=wd, fc=fc: e.dma_start(
                out=wd[:, fc, :], in_=w_down[fb * 512 + fc * 128:fb * 512 + (fc + 1) * 128, :]), writes=[wdn])
        for fc in range(4):
            for (t0, tn) in tchunks:
                bank = pb % 8
                pb += 1
                pp = psum(bank, 1)[:, 0:tn]
                pn = pnames(bank)
                for kc in range(16):
                    fw.op("pe", lambda e, kc=kc, pp=pp, fc=fc, t0=t0, tn=tn, wu=wu: e.matmul(
                        pp, wu[:, kc, fc * 128:(fc + 1) * 128], h2T[:, kc, t0:t0 + tn],
                        start=(kc == 0), stop=(kc == 15)), reads=[n_h2T, wun], writes=pn)
                r, rn = rl[ri % 2][0][:, 0:tn], rl[ri % 2][1]
                ri += 1
                fw.op("act", lambda e, r=r, pp=pp: e.activation(out=r, in_=pp, func=AF.Relu), reads=pn, writes=[rn])
                fw.op("pool", lambda e, r=r, fc=fc, t0=t0, tn=tn, hd=hd: e.tensor_tensor(
                    out=hd[:, fc, t0:t0 + tn], in0=r, in1=r, op=ALU.mult), reads=[rn], writes=[hn])
        for ti in range(9):
            b0 = (pb % 2) * 4
            pb += 4
            po = psum(b0, 4)
            pns = pnames(b0, 4)
            for nch in range(4):
                for fc in range(4):
                    fw.op("pe", lambda e, nch=nch, fc=fc, po=po, ti=ti, hd=hd, wd=wd: e.matmul(
                        po[:, nch * 512:(nch + 1) * 512], hd[:, fc, ti * 128:(ti + 1) * 128],
                        wd[:, fc, nch * 512:(nch + 1) * 512], start=(fc == 0), stop=(fc == 3)),
                        reads=[hn, wdn], writes=[pns[nch]])
            ya = yacc[:, ti, :]
            for nch in range(4):
                fw.op("dve", lambda e, ya=ya, po=po, nch=nch: e.tensor_tensor(
                    out=ya[:, nch * 512:(nch + 1) * 512], in0=po[:, nch * 512:(nch + 1) * 512],
                    in1=ya[:, nch * 512:(nch + 1) * 512], op=ALU.add),
                    reads=[pns[nch], n_yacc0], writes=[n_yacc0])
            if fb == NFB - 1:
                fw.dma("sp", "yout", lambda e, ya=ya, ti=ti: e.dma_start(out=y_d[ti * 128:(ti + 1) * 128, :], in_=ya),
                       reads=[n_yacc0], writes=["y_out"])

    fw.stopped = False
    fw.final_wait("sp", ["y_out", "ko_out", "vo_out", "uo_out"])
    fw.emit()
    stack.close()
    return nc


_CACHE = {}
_STOP = None
_DBG_NB = None


def kernel(x_prompt, x_sample, state_k, state_v, state_conv,
           norm1_g, w_in, q_norm_g, k_norm_g, conv_w, w_out, norm2_g, w_up, w_down):
    f32 = np.float32
    xp = np.asarray(x_prompt, f32)[0]
    xs = np.asarray(x_sample, f32).reshape(NCORES, TS, D)
    skk = np.asarray(state_k, f32)[0].reshape(128, 2048, AW)
    svv = np.asarray(state_v, f32)[0].reshape(128, 2048, AW)
    scc = np.asarray(state_conv, f32)[0].reshape(128 * 2, AW)
    Wp, keep, Wnear, Wfar, Wnew = _tables()
    if "nc" not in _CACHE:
        _CACHE["nc"] = build_program(keep, _STOP)
    nc = _CACHE["nc"]

    w_in0 = np.ascontiguousarray(np.asarray(w_in, f32)[0])
    w_out0 = np.ascontiguousarray(np.asarray(w_out, f32)[0])
    w_up0 = np.ascontiguousarray(np.asarray(w_up, f32)[0])
    w_down0 = np.ascontiguousarray(np.asarray(w_down, f32)[0])
    g1b = np.ascontiguousarray(np.broadcast_to(np.asarray(norm1_g, f32)[0][None, :], (128, D)))
    g2b = np.ascontiguousarray(np.broadcast_to(np.asarray(norm2_g, f32)[0][None, :], (128, D)))
    qgb = np.ascontiguousarray(np.broadcast_to(np.tile(np.asarray(q_norm_g, f32)[0], NH)[None, :], (128, AW)))
    kgb = np.ascontiguousarray(np.broadcast_to(np.tile(np.asarray(k_norm_g, f32)[0], NH)[None, :], (128, AW)))
    cwT = np.ascontiguousarray(np.asarray(conv_w, f32)[0].reshape(3, 8, 128).transpose(2, 1, 0))
    ident = np.eye(128, dtype=f32)

    in_maps = []
    for c in range(NCORES):
        c0 = c * TP
        xh = np.zeros((NHALO, D), f32)
        lo = max(0, c0 - NHALO)
        if c0 > 0:
            xh[NHALO - (c0 - lo):] = xp[lo:c0]
        x2 = np.zeros((128, D), f32)
        x2[0:2] = xh[NHALO - 2:NHALO]
        xo = np.concatenate([xp[c0:c0 + TP], xs[c]], axis=0)
        tok = c0 - NHALO + np.arange(24 * 128)
        valid = np.ascontiguousarray((tok >= 0).astype(f32).reshape(24, 128).T)
        in_maps.append({
            "xh": xh, "xo": np.ascontiguousarray(xo), "x2": x2,
            "sk": skk[c * NB:(c + 1) * NB], "sv": svv[c * NB:(c + 1) * NB],
            "sc": scc[c * 2 * NB:(c + 1) * 2 * NB],
            "w_in": w_in0, "w_out": w_out0, "w_up": w_up0, "w_down": w_down0,
            "g1b": g1b, "g2b": g2b, "qgb": qgb, "kgb": kgb, "cwT": cwT, "valid": valid,
            "ident": ident, "Wp": Wp, "Wnear": Wnear, "Wfar": Wfar, "Wnew": Wnew,
        })
    res = run_bass_kernel_spmd(nc, in_maps, core_ids=list(range(NCORES)))
    R = res.results
    y = np.stack([np.asarray(r["y"], f32) for r in R])
    ko = np.stack([np.asarray(r["ko"], f32) for r in R])
    vo = np.stack([np.asarray(r["vo"], f32) for r in R])
    uo = np.stack([np.asarray(r["uo"], f32) for r in R])
    y_prompt = y[:, :TP].reshape(1, 8192, D)
    y_sample = y[:, TP:].reshape(128, 8, D)
    nkp = ko[6:8, :TP].reshape(1, 1, 2048, NH, HD)
    nvp = vo[6:8, :TP].reshape(1, 1, 2048, NH, HD)
    ncp = uo[7, 0:2].reshape(1, 1, 2, AW)
    nks = ko[:, TP:].reshape(1, 128, 8, NH, HD)
    nvs = vo[:, TP:].reshape(1, 128, 8, NH, HD)
    ncs = uo[:, 2:34].reshape(1, 128, 2, AW)
    return (np.ascontiguousarray(y_prompt), np.ascontiguousarray(y_sample), np.ascontiguousarray(nkp),
            np.ascontiguousarray(nvp), np.ascontiguousarray(ncp), np.ascontiguousarray(nks),
            np.ascontiguousarray(nvs), np.ascontiguousarray(ncs))
```

```python
import numpy as np
from contextlib import ExitStack
import concourse.bass as bass
import concourse.mybir as mybir
from concourse.bass_utils import run_bass_kernel_spmd

F32 = mybir.dt.float32
BF16 = mybir.dt.bfloat16
U8 = mybir.dt.uint8
ALU = mybir.AluOpType
AF = mybir.ActivationFunctionType
AX = mybir.AxisListType

NCORES = 8
D = 2048
DFF = 8192
NH = 16
HD = 64
AW = 1024
EPS = 1e-6
TP = 1024
TS = 128
TO = TP + TS
NHALO = 2048
NB = 16
SKIP_THRESH = 1e-30


class FW:
    ENG = ("pe", "act", "dve", "pool", "sp")

    def __init__(self, nc, stack):
        self.nc = nc
        self.stack = stack
        self.ops = {e: [] for e in self.ENG}
        self.cnt = {}
        self.sems = {}
        self.lastw = {}
        self.readers = {}
        self.iv = {}
        self.ovl = {}
        self.seen = {e: {} for e in self.ENG}
        self.stopped = False
        self.nrec = 0
        self.stop_n = None
        for e in self.ENG:
            self._sem("eng_" + e)

    def register(self, name, space, start, end):
        self.iv[name] = (space, start, end)
        self.ovl = {}

    def _over(self, name):
        o = self.ovl.get(name)
        if o is None:
            o = [name]
            if name in self.iv:
                sp, a, b = self.iv[name]
                for m, (sp2, a2, b2) in self.iv.items():
                    if m != name and sp2 == sp and a2 < b and a < b2:
                        o.append(m)
            self.ovl[name] = o
        return o

    def _sem(self, key):
        if key not in self.sems:
            self.sems[key] = self.stack.enter_context(self.nc.semaphore("s_" + key))
            self.cnt[key] = 0
        return key

    def _deps(self, eng, reads, writes, skip=None):
        ev = []
        for b in reads:
            for m in self._over(b):
                w = self.lastw.get(m)
                if w is not None:
                    ev.append(w)
        for b in writes:
            for m in self._over(b):
                w = self.lastw.get(m)
                if w is not None:
                    ev.append(w)
                ev.extend(self.readers.get(m, ()))
        waits = {}
        for (k, v) in ev:
            if (eng == "pe" and k == "eng_pe") or k == skip:
                continue
            if v > waits.get(k, 0):
                waits[k] = v
        out = []
        for k, v in waits.items():
            if self.seen[eng].get(k, 0) >= v:
                continue
            self.seen[eng][k] = v
            out.append((k, v))
        return out

    def _commit(self, event, reads, writes):
        for b in reads:
            self.readers.setdefault(b, []).append(event)
        for b in writes:
            self.lastw[b] = event
            self.readers[b] = []

    def op(self, eng, fn, reads=(), writes=()):
        self.nrec += 1
        if self.stop_n is not None and self.nrec > self.stop_n:
            self.stopped = True
        if self.stopped:
            return
        reads, writes = list(reads), list(writes)
        writes = writes + [r for r in reads if r.startswith("ps") and r not in writes]
        reads = [r for r in reads if not r.startswith("ps")]
        waits = self._deps(eng, reads, writes)
        k = "eng_" + eng
        self.cnt[k] += 1
        ev = (k, self.cnt[k])
        self.ops[eng].append((waits, fn, k, 1))
        self._commit(ev, reads, writes)

    def dma(self, eng, key, fn, reads=(), writes=()):
        self.nrec += 1
        if self.stop_n is not None and self.nrec > self.stop_n:
            self.stopped = True
        if self.stopped:
            return
        reads, writes = list(reads), list(writes)
        k = self._sem("d_" + key)
        waits = self._deps(eng, reads, writes, skip=k)
        self.cnt[k] += 16
        ev = (k, self.cnt[k])
        self.ops[eng].append((waits, fn, k, 16))
        self._commit(ev, reads, writes)

    def final_wait(self, eng, bufs):
        waits = self._deps(eng, list(bufs), ())
        self.ops[eng].append((waits, None, None, 0))

    def emit(self):
        nc = self.nc
        with nc.Block() as block:
            def mk(e):
                def body(engine):
                    for (waits, fn, k, inc) in self.ops[e]:
                        for (wk, wv) in waits:
                            engine.wait_ge(self.sems[wk], wv)
                        if fn is not None:
                            fn(engine).then_inc(self.sems[k], inc)
                return body
            block.tensor(mk("pe"))
            block.scalar(mk("act"))
            block.vector(mk("dve"))
            block.gpsimd(mk("pool"))
            block.sync(mk("sp"))


def _mult(d):
    d = np.asarray(d)
    m = ((d >= 0) & (d <= 128)).astype(np.float64)
    m += ((d >= 0) & (d <= 512) & (d % 4 == 0))
    m += ((d >= 0) & (d <= 2048) & (d % 16 == 0))
    return m


def _slopes():
    return 2.0 ** (-8.0 * np.arange(1, NH + 1, dtype=np.float64) / NH)


def _tables():
    sl = _slopes()
    b = np.arange(128)[:, None]
    a = np.arange(128)[None, :]
    Wp = np.zeros((8, 128, 17, 256), np.float32)
    for dl in range(17):
        d = 128 * dl + a - b
        m = _mult(d)
        for h in range(NH):
            w = m * np.exp(-sl[h] * np.maximum(d, 0))
            Wp[h // 2, :, dl, (h % 2) * 128:(h % 2) * 128 + 128] = w
    keep = [[dl for dl in range(17) if Wp[p, :, dl, :].max() > SKIP_THRESH] for p in range(8)]
    hh = np.repeat(np.arange(NH), 8)[None, :]
    tt = np.tile(np.arange(8), NH)[None, :]
    Wnear = np.zeros((128, 4, 128), np.float32)
    for kt in range(4):
        row = 1536 + 128 * kt + np.arange(128)[:, None]
        d = 2048 + tt - row
        Wnear[:, kt, :] = _mult(d) * np.exp(-sl[hh] * np.maximum(d, 0))
    Wfar = np.zeros((128, 8, 128), np.float32)
    for r in range(8):
        row = 16 * np.arange(96)[:, None] + r
        d = 2048 + tt - row
        w = _mult(d) * np.exp(-sl[hh] * np.maximum(d, 0))
        w = w * (tt == r)
        Wfar[:96, r, :] = w
    Wnew = np.zeros((128, NB, 128), np.float32)
    kb = (np.arange(128) // 8)[:, None]
    kt_ = (np.arange(128) % 8)[:, None]
    for bb in range(NB):
        d = tt - kt_
        Wnew[:, bb, :] = (kb == bb) * _mult(d) * np.exp(-sl[hh] * np.maximum(d, 0))
    return Wp, keep, Wnear, Wfar, Wnew


def build_program(keep, stop=None, small=False):
    nc = bass.Bass("TRN2", target_bir_lowering=False)
    stack = ExitStack()

    def din(name, shape, dt=F32):
        return nc.dram_tensor(name, list(shape), dt, kind="ExternalInput").ap()

    def dout(name, shape, dt=F32):
        return nc.dram_tensor(name, list(shape), dt, kind="ExternalOutput").ap()

    xh = din("xh", [NHALO, D])
    xo = din("xo", [TO, D])
    x2 = din("x2", [128, D])
    sk = din("sk", [NB, 2048, AW] if not small else [1, 16, AW])
    sv = din("sv", [NB, 2048, AW] if not small else [1, 16, AW])
    sc = din("sc", [2 * NB, AW])
    w_in = din("w_in", [D, 6144])
    w_out = din("w_out", [D, D])
    w_up = din("w_up", [D, DFF] if not small else [128, 128])
    w_down = din("w_down", [DFF, D] if not small else [128, 128])
    g1b = din("g1b", [128, D])
    g2b = din("g2b", [128, D])
    qgb = din("qgb", [128, AW])
    kgb = din("kgb", [128, AW])
    cwT = din("cwT", [128, 8, 3])
    valid = din("valid", [128, 24])
    identd = din("ident", [128, 128])
    Wp_d = din("Wp", [8, 128, 17, 256])
    Wnear_d = din("Wnear", [128, 4, 128])
    Wfar_d = din("Wfar", [128, 8, 128])
    Wnew_d = din("Wnew", [128, NB, 128])

    y_d = dout("y", [TO, D])
    ko_d = dout("ko", [TO, AW])
    vo_d = dout("vo", [TO, AW])
    uo_d = dout("uo", [34, AW])

    KB = 1024
    SB_BYTES = 207 * KB
    big = nc.alloc_sbuf_tensor("big", [128, SB_BYTES], U8)
    pbig = stack.enter_context(nc.psum_tensor("pbig", [128, 8 * 512], F32))
    fw = FW(nc, stack)
    if isinstance(stop, int):
        fw.stop_n = stop
    for bnk in range(8):
        fw.register("ps%d" % bnk, "psum", bnk * 2048, (bnk + 1) * 2048)

    class Arena:
        def __init__(self):
            self.off = 0
            self.uid = 0

        def seek(self, off):
            self.off = off

        def take(self, name, shape, dt):
            nb = int(np.prod(shape)) * mybir.dt.size(dt)
            nb_al = (nb + 63) // 64 * 64
            off = self.off
            self.off += nb_al
            assert off + nb_al <= SB_BYTES, ("SBUF overflow", name, off, nb_al)
            self.uid += 1
            uname = "%s#%d" % (name, self.uid)
            fw.register(uname, "sbuf", off, off + nb_al)
            v = big[:, off:off + nb].bitcast(dt)
            if len(shape) == 2:
                v = v.rearrange("p (a b) -> p a b", a=shape[0])
            elif len(shape) == 3:
                v = v.rearrange("p (a b c) -> p a b c", a=shape[0], b=shape[1])
            return v, uname

    def psum(bank, nbanks=1, dt=F32, shape=None):
        v = pbig[:, bank * 512:(bank + nbanks) * 512]
        if dt != F32:
            v = v.bitcast(dt)
        if shape is not None:
            v = v.rearrange("p (a b) -> p a b", a=shape[0])
        return v

    def wdma(key, wt, wn, src2d):
        for kc in range(16):
            fw.dma("pool", key, lambda e, kc=kc: e.dma_start(out=wt[:, kc, :], in_=src2d[kc * 128:(kc + 1) * 128, :]),
                   writes=[wn])

    def pnames(bank, n=1):
        return ["ps%d" % (bank + j) for j in range(n)]

    ar = Arena()
    R_C, R_A, R_B = 17 * KB, 53 * KB, 165 * KB
    ident, n_ident = ar.take("ident", [128], BF16)
    identf, n_identf = ar.take("identf", [128], F32)
    ones, n_ones = ar.take("ones", [128], BF16)
    valid_sb, n_valid = ar.take("valid", [24], F32)
    cw_sb, n_cw = ar.take("cw", [8, 3], F32)
    stat, n_stat = ar.take("stat", [16], F32)
    nst, n_nst = ar.take("nst", [16], F32)
    KTnew, n_KTnew = ar.take("KTnew", [8, 128], BF16)
    Vnew, n_Vnew = ar.take("Vnew", [AW], BF16)
    QTs, n_QTs = ar.take("QTs", [8, 128], BF16)
    assert ar.off <= 8 * KB, ar.off
    ar.seek(8 * KB)
    gS, n_gS = ar.take("gS", [9 * 256], F32)
    assert ar.off <= R_C
    ar.seek(R_C)
    catT, n_catT = ar.take("catT", [16, TO], BF16)
    g1 = gS[:, 0:D]

    fw.dma("pool", "c_ident", lambda e: e.dma_start(out=ident, in_=identd), writes=[n_ident])
    fw.dma("sp", "c_identf", lambda e: e.dma_start(out=identf, in_=identd), writes=[n_identf])
    fw.dma("sp", "c_valid", lambda e: e.dma_start(out=valid_sb, in_=valid), writes=[n_valid])
    fw.dma("sp", "c_cw", lambda e: e.dma_start(out=cw_sb, in_=cwT), writes=[n_cw])
    fw.dma("sp", "c_g1", lambda e: e.dma_start(out=g1, in_=g1b), writes=[n_gS])
    fw.op("dve", lambda e: e.memset(ones, 1.0), writes=[n_ones])

    def norm_transpose(src_ap, gb, n_gb, xt, n_xt, hb, n_hb, hT_dst, n_hT, slot, ps_bank):
        fw.dma("sp", "x%d" % slot, lambda e: e.dma_start(out=xt, in_=src_ap), writes=[n_xt])
        st = stat[:, 4 * slot:4 * slot + 4]
        fw.op("act", lambda e: e.activation(out=hb, in_=xt, func=AF.Square, accum_out=st[:, 0:1]),
              reads=[n_xt], writes=[n_hb, n_stat])
        fw.op("dve", lambda e: e.tensor_scalar(st[:, 1:2], st[:, 0:1], 1.0 / D, EPS, ALU.mult, ALU.add),
              reads=[n_stat], writes=[n_stat])
        fw.op("dve", lambda e: e.reciprocal(st[:, 2:3], st[:, 1:2]), reads=[n_stat], writes=[n_stat])
        fw.op("act", lambda e: e.activation(out=st[:, 3:4], in_=st[:, 2:3], func=AF.Sqrt),
              reads=[n_stat], writes=[n_stat])
        fw.op("dve", lambda e: e.scalar_tensor_tensor(out=hb, in0=xt, scalar=st[:, 3:4], in1=gb,
                                                      op0=ALU.mult, op1=ALU.mult),
              reads=[n_xt, n_stat, n_gb], writes=[n_hb])
        pT = psum(ps_bank, 2, BF16, [16, 128])
        pn = pnames(ps_bank, 2)
        for kc in range(16):
            fw.op("pe", lambda e, kc=kc: e.transpose(pT[:, kc, :], hb[:, kc * 128:(kc + 1) * 128], ident),
                  reads=[n_hb, n_ident], writes=pn)
        fw.op("act", lambda e: e.copy(hT_dst[:, 0:8, :], pT[:, 0:8, :]), reads=[pn[0]], writes=[n_hT])
        fw.op("dve", lambda e: e.tensor_copy(hT_dst[:, 8:16, :], pT[:, 8:16, :]), reads=[pn[1]], writes=[n_hT])

    ar.seek(R_A)
    KT, n_KT = ar.take("KT", [8, 3072], BF16)
    Vb, n_Vb = ar.take("Vb", [24, AW], BF16)
    QT, n_QT = ar.take("QT", [8, TP], BF16)
    assert ar.off <= R_B, ar.off
    ar.seek(R_C)
    wblk = [ar.take("wblk%d" % i, [16, 512], BF16) for i in range(2)]
    hg, n_hg = ar.take("hg", [AW], F32)
    assert ar.off <= R_A
    ar.seek(R_B)
    xts = [ar.take("xt%d" % i, [D], F32) for i in range(2)]
    hbs = [ar.take("hb%d" % i, [D], BF16) for i in range(2)]
    hTs = [ar.take("hT%d" % i, [16, 128], BF16) for i in range(2)]
    tmpf, n_tmpf = ar.take("tmpf", [AW], F32)
    kst, n_kst = ar.take("kst", [AW], F32)
    knb, n_knb = ar.take("knb", [AW], BF16)
    assert ar.off <= SB_BYTES

    def head_norm(ps_ap, psn, out_f32, out_bf, scale):
        ns = nst[:, 0:16]
        for hf in range(2):
            fw.op("act", lambda e, hf=hf: e.activation(out=tmpf[:, hf * 512:(hf + 1) * 512],
                                                       in_=ps_ap[:, hf * 512:(hf + 1) * 512], func=AF.Square),
                  reads=[psn[hf]], writes=[n_tmpf])
        fw.op("dve", lambda e: e.tensor_reduce(out=ns, in_=tmpf.rearrange("p (h d) -> p h d", d=HD),
                                               axis=AX.X, op=ALU.add), reads=[n_tmpf], writes=[n_nst])
        fw.op("dve", lambda e: e.tensor_scalar(ns, ns, 1.0 / HD, EPS, ALU.mult, ALU.add), reads=[n_nst], writes=[n_nst])
        fw.op("dve", lambda e: e.reciprocal(ns, ns), reads=[n_nst], writes=[n_nst])
        fw.op("act", lambda e: e.activation(out=ns, in_=ns, func=AF.Sqrt), reads=[n_nst], writes=[n_nst])
        for hf in range(2):
            fw.op("dve", lambda e, hf=hf: e.tensor_tensor(
                out=tmpf[:, hf * 512:(hf + 1) * 512].rearrange("p (h d) -> p h d", d=HD),
                in0=ps_ap[:, hf * 512:(hf + 1) * 512].rearrange("p (h d) -> p h d", d=HD),
                in1=ns[:, hf * 8:(hf + 1) * 8].unsqueeze(2).to_broadcast([128, 8, HD]), op=ALU.mult),
                reads=[psn[hf], n_nst], writes=[n_tmpf])
        fw.op("dve", lambda e: e.tensor_tensor(out=out_f32, in0=tmpf, in1=hg, op=ALU.mult),
              reads=[n_tmpf, n_hg], writes=[n_kst])
        fw.op("act", lambda e: e.activation(out=out_bf, in_=out_f32, func=AF.Copy, scale=scale),
              reads=[n_kst], writes=[n_knb])

    tcount = [0]

    def transpose8(dst_ap, n_dst, bank):
        pT = psum(bank, 1, BF16, [8, 128])
        pn = pnames(bank)
        for p in range(8):
            fw.op("pe", lambda e, p=p: e.transpose(pT[:, p, :], knb[:, p * 128:(p + 1) * 128], ident),
                  reads=[n_knb, n_ident], writes=pn)
        fw.op("act", lambda e: e.copy(dst_ap, pT), reads=pn, writes=[n_dst])

    for (pname, wcol, tiles) in (("k", 1024, range(25)), ("v", 2048, range(25)), ("q", 0, range(16, 25))):
        for i in range(2):
            wdma("wblk%d" % i, wblk[i][0], wblk[i][1], w_in[:, wcol + i * 512:wcol + (i + 1) * 512])
        if pname == "k":
            fw.dma("sp", "hg", lambda e: e.dma_start(out=hg, in_=kgb), writes=[n_hg])
        elif pname == "q":
            fw.dma("sp", "hg", lambda e: e.dma_start(out=hg, in_=qgb), writes=[n_hg])
        for ti in tiles:
            slot = ti % 2
            src = xh[ti * 128:(ti + 1) * 128, :] if ti < 16 else xo[(ti - 16) * 128:(ti - 15) * 128, :]
            own = ti >= 16
            r0 = (ti - 16) * 128
            hT, n_hT = hTs[slot]
            norm_transpose(src, g1, n_gS, xts[slot][0], xts[slot][1], hbs[slot][0], hbs[slot][1],
                           hT, n_hT, slot, 2 * slot)
            pkb = 4 + 2 * (ti % 2)
            pk = psum(pkb, 2)
            pkn = pnames(pkb, 2)
            for nch in range(2):
                for kc in range(16):
                    fw.op("pe", lambda e, nch=nch, kc=kc, hT=hT, pk=pk: e.matmul(
                        pk[:, nch * 512:(nch + 1) * 512], hT[:, kc, :], wblk[nch][0][:, kc, :],
                        start=(kc == 0), stop=(kc == 15)),
                        reads=[n_hT, wblk[nch][1]], writes=[pkn[nch]])
            if pname == "v":
                vdst, n_vdst = (Vb[:, ti, :], n_Vb) if ti < 24 else (Vnew, n_Vnew)
                for hf in range(2):
                    fw.op("dve", lambda e, vdst=vdst, hf=hf, pk=pk: e.tensor_copy(
                        vdst[:, hf * 512:(hf + 1) * 512], pk[:, hf * 512:(hf + 1) * 512]),
                        reads=[pkn[hf]], writes=[n_vdst])
                if own:
                    for hf in range(2):
                        fw.op("act", lambda e, hf=hf, pk=pk: e.copy(kst[:, hf * 512:(hf + 1) * 512],
                                                             pk[:, hf * 512:(hf + 1) * 512]),
                              reads=[pkn[hf]], writes=[n_kst])
                    fw.dma("sp", "vo", lambda e, r0=r0: e.dma_start(out=vo_d[r0:r0 + 128, :], in_=kst),
                           reads=[n_kst], writes=["vo_out"])
            elif pname == "k":
                head_norm(pk, pkn, kst, knb, 1.0)
                if own:
                    fw.dma("sp", "ko", lambda e, r0=r0: e.dma_start(out=ko_d[r0:r0 + 128, :], in_=kst),
                           reads=[n_kst], writes=["ko_out"])
                if ti < 24:
                    transpose8(KT[:, :, ti * 128:(ti + 1) * 128], n_KT, pkb)
                else:
                    transpose8(KTnew, n_KTnew, pkb)
            else:
                head_norm(pk, pkn, kst, knb, HD ** -0.5)
                if ti < 24:
                    transpose8(QT[:, :, r0:r0 + 128], n_QT, pkb)
                else:
                    transpose8(QTs, n_QTs, pkb)

    if stop == "kv":
        fw.stopped = True
    ar.seek(R_B)
    Wps = [ar.take("Wp%d" % i, [17, 256], BF16) for i in range(2)]
    Es = [ar.take("E%d" % i, [256], F32) for i in range(3)]
    Ps = [ar.take("P%d" % i, [256], BF16) for i in range(3)]
    rden, n_rden = ar.take("rden", [256], F32)
    vones, n_vones = ar.take("vones", [16, 128], BF16)
    fw.op("dve", lambda e: e.tensor_copy(vones, valid_sb[:, 0:16].unsqueeze(2).to_broadcast([128, 16, 128])),
          reads=[n_valid], writes=[n_vones])
    QTbd = [ar.take("QTbd%d" % i, [8, 256], BF16) for i in range(2)]
    for i in range(2):
        fw.op("dve", lambda e, i=i: e.memset(QTbd[i][0], 0.0), writes=[QTbd[i][1]])
    tl = []
    for p in range(8):
        dls = keep[p]
        for qb in range(8):
            for i, dl in enumerate(dls):
                tl.append(dict(p=p, qb=qb, i=i, dl=dl, n=len(dls), kt=16 + qb - dl, idx=len(tl)))
    DEPTH = 2
    NR = DEPTH + 1
    Es = Es[:NR]
    Ps = Ps[:NR]

    def front(t):
        p, qb, dl, kt, j = t["p"], t["qb"], t["dl"], t["kt"], t["idx"]
        Qb, Qbn = QTbd[p % 2]
        Wt, Wn = Wps[p % 2]
        if qb == 0 and t["i"] == 0:
            for hh in range(2):
                fw.op("dve", lambda e, hh=hh, p=p, Qb=Qb: e.tensor_copy(
                    Qb[hh * 64:(hh + 1) * 64, :, hh * 128:(hh + 1) * 128],
                    QT[hh * 64:(hh + 1) * 64, p, :].rearrange("p (q a) -> p q a", a=128)),
                    reads=[n_QT], writes=[Qbn])
            fw.dma("pool", "Wp%d" % (p % 2), lambda e, p=p, Wt=Wt: e.dma_start(out=Wt, in_=Wp_d[p]), writes=[Wn])
        kcol = kt * 128
        sbank = 2 + (j % NR)
        S = psum(sbank, 1)[:, 0:256]
        Sn = pnames(sbank)
        E, En = Es[j % NR]
        P, Pn = Ps[j % NR]
        fw.op("pe", lambda e, S=S, kcol=kcol, qb=qb, p=p, Qb=Qb: e.matmul(
            S, KT[:, p, kcol:kcol + 128], Qb[:, qb, :], start=True, stop=True),
            reads=[n_KT, Qbn], writes=Sn)
        fw.op("act", lambda e, E=E, S=S: e.activation(out=E, in_=S, func=AF.Exp), reads=Sn, writes=[En])
        fw.op("dve", lambda e, P=P, E=E, dl=dl, Wt=Wt: e.tensor_tensor(
            out=P, in0=E, in1=Wt[:, dl, :], op=ALU.mult), reads=[En, Wn], writes=[Pn])

    def back(t):
        p, qb, kt, j = t["p"], t["qb"], t["kt"], t["idx"]
        P, Pn = Ps[j % NR]
        accb = qb % 2
        acc = psum(accb, 1)
        accd = psum(5 + accb, 1)
        accn = pnames(accb)
        accdn = pnames(5 + accb)
        qcol = qb * 128
        first = (t["i"] == 0)
        last = (t["i"] == t["n"] - 1)
        fw.op("pe", lambda e, P=P, kt=kt, p=p, acc=acc, first=first, last=last: e.matmul(
            acc[:, 0:256], Vb[:, kt, p * 128:(p + 1) * 128], P, start=first, stop=last),
            reads=[Pn, n_Vb], writes=accn)
        onesv = vones[:, kt, :] if kt < 16 else ones
        fw.op("pe", lambda e, P=P, accd=accd, first=first, last=last, onesv=onesv: e.matmul(
            accd[:, 0:256], onesv, P, start=first, stop=last),
            reads=[Pn, n_ones, n_vones], writes=accdn)
        if last:
            fw.op("dve", lambda e, accd=accd: e.reciprocal(rden, accd[:, 0:256]), reads=accdn, writes=[n_rden])
            fw.op("dve", lambda e, acc=acc, p=p, qcol=qcol: e.tensor_tensor(
                out=catT[0:64, p, qcol:qcol + 128], in0=acc[0:64, 0:128], in1=rden[0:64, 0:128], op=ALU.mult),
                reads=accn + [n_rden], writes=[n_catT])
            fw.op("dve", lambda e, acc=acc, p=p, qcol=qcol: e.tensor_tensor(
                out=catT[64:128, p, qcol:qcol + 128], in0=acc[64:128, 128:256], in1=rden[64:128, 128:256],
                op=ALU.mult), reads=accn + [n_rden], writes=[n_catT])

    for j in range(len(tl) + DEPTH):
        if j < len(tl):
            front(tl[j])
        if j - DEPTH >= 0:
            back(tl[j - DEPTH])

    if stop == "attp":
        fw.stopped = True
    ar.seek(R_A)
    Qbd, n_Qbd = ar.take("Qbd", [8, NB, 16], BF16)
    Wnear, n_Wnear = ar.take("Wnear", [4, 128], BF16)
    Wfar, n_Wfar = ar.take("Wfar", [8, 128], BF16)
    Wnew, n_Wnew = ar.take("Wnew", [NB, 128], BF16)
    kfar = [ar.take("kfar%d" % i, [8, AW], BF16) for i in range(2)]
    knear = [ar.take("knear%d" % i, [4, AW], BF16) for i in range(2)]
    vfar = [ar.take("vfar%d" % i, [8, AW], BF16) for i in range(2)]
    vnear = [ar.take("vnear%d" % i, [4, AW], BF16) for i in range(2)]
    KTs = [ar.take("KTs%d" % i, [8, 128], BF16) for i in range(3)]
    Es2 = [ar.take("E2_%d" % i, [128], F32) for i in range(3)]
    Pall, n_Pall = ar.take("Pall", [13, 128], BF16)
    rden2, n_rden2 = ar.take("rden2", [128], F32)
    osb, n_osb = ar.take("osb", [128], F32)

    fw.dma("pool", "Wnear", lambda e: e.dma_start(out=Wnear, in_=Wnear_d), writes=[n_Wnear])
    fw.dma("pool", "Wfar", lambda e: e.dma_start(out=Wfar, in_=Wfar_d), writes=[n_Wfar])
    fw.dma("pool", "Wnew", lambda e: e.dma_start(out=Wnew, in_=Wnew_d), writes=[n_Wnew])
    fw.op("dve", lambda e: e.memset(Qbd, 0.0), writes=[n_Qbd])
    for hh in range(2):
        fw.op("dve", lambda e, hh=hh: e.tensor_copy(
            Qbd[hh * 64:(hh + 1) * 64, :, :, hh * 8:(hh + 1) * 8],
            QTs[hh * 64:(hh + 1) * 64, :, :].rearrange("p a (b t) -> p a b t", t=8)),
            reads=[n_QTs], writes=[n_Qbd])

    it = 0
    for b in range(NB if _DBG_NB is None else _DBG_NB):
        s = b % 2
        for (far_t, near_t, srcd, nm) in ((kfar[s], knear[s], sk, "k"), (vfar[s], vnear[s], sv, "v")):
            for r in range(8):
                fw.dma("pool", "%sfar%d" % (nm, s), lambda e, b=b, r=r, far_t=far_t, srcd=srcd: e.dma_start(
                    out=far_t[0][0:96, r, :],
                    in_=srcd[b, 0:1536, :].rearrange("(j r) c -> j r c", r=16)[:, r, :]), writes=[far_t[1]])
            for t in range(4):
                fw.dma("pool", "%snear%d" % (nm, s), lambda e, b=b, t=t, near_t=near_t, srcd=srcd: e.dma_start(
                    out=near_t[0][:, t, :], in_=srcd[b, 1536 + t * 128:1536 + (t + 1) * 128, :]), writes=[near_t[1]])
        accb = b % 2
        accO = psum(accb, 1)[:, 0:128]
        accD = psum(7, 1)[:, 0:128]
        accn = pnames(accb)
        accdn2 = pnames(7)
        vlist = []
        tiles = [("far", r) for r in range(8)] + [("near", k) for k in range(4)] + [("new", 0)]
        for i, (kind, j) in enumerate(tiles):
            nk = 96 if kind == "far" else 128
            if kind == "far":
                ksrc, ksn = kfar[s][0][0:96, j, :], kfar[s][1]
                vsrc, vsn = vfar[s][0][0:96, j, :], vfar[s][1]
                W, Wn = Wfar[0:96, j, :], n_Wfar
            elif kind == "near":
                ksrc, ksn = knear[s][0][:, j, :], knear[s][1]
                vsrc, vsn = vnear[s][0][:, j, :], vnear[s][1]
                W, Wn = Wnear[:, j, :], n_Wnear
            else:
                ksrc, ksn = None, None
                vsrc, vsn = Vnew, n_Vnew
                W, Wn = Wnew[:, b, :], n_Wnew
            r3 = it % 3
            it += 1
            if kind != "new":
                KTt, KTn = KTs[r3]
                tb = 5 + (it % 2)
                pT = psum(tb, 1, BF16, [8, 128])
                for p in range(8):
                    fw.op("pe", lambda e, p=p, pT=pT, ksrc=ksrc, nk=nk: e.transpose(
                        pT[:, p, 0:nk], ksrc[:, p * 128:(p + 1) * 128], ident[0:nk, 0:nk]),
                        reads=[ksn, n_ident], writes=pnames(tb))
                fw.op("dve", lambda e, KTt=KTt, pT=pT, nk=nk: e.tensor_copy(KTt[:, :, 0:nk], pT[:, :, 0:nk]),
                      reads=pnames(tb), writes=[KTn])
                ktv = lambda p, KTt=KTt, nk=nk: KTt[:, p, 0:nk]
            else:
                KTn = n_KTnew
                ktv = lambda p: KTnew[:, p, :]
            sbank = 2 + r3
            S = psum(sbank, 1)[0:nk, 0:128]
            Sn = pnames(sbank)
            for p in range(8):
                fw.op("pe", lambda e, p=p, S=S, ktv=ktv, b=b: e.matmul(
                    S[:, p * 16:(p + 1) * 16], ktv(p), Qbd[:, p, b, :], start=True, stop=True),
                    reads=[KTn, n_Qbd], writes=Sn)
            E, En = Es2[r3][0][0:nk], Es2[r3][1]
            P = Pall[0:nk, i, :]
            fw.op("act", lambda e, E=E, S=S: e.activation(out=E, in_=S, func=AF.Exp), reads=Sn, writes=[En])
            fw.op("dve", lambda e, P=P, E=E, W=W: e.tensor_tensor(out=P, in0=E, in1=W, op=ALU.mult),
                  reads=[En, Wn], writes=[n_Pall])
            vlist.append((vsrc, vsn, nk))
        nt = len(tiles)
        for p in range(8):
            for i, (vsrc, vsn, nk) in enumerate(vlist):
                fw.op("pe", lambda e, p=p, i=i, vsrc=vsrc, nk=nk, accO=accO: e.matmul(
                    accO[:, p * 16:(p + 1) * 16], vsrc[:, p * 128:(p + 1) * 128], Pall[0:nk, i, p * 16:(p + 1) * 16],
                    start=(i == 0), stop=(i == nt - 1)), reads=[n_Pall, vsn], writes=accn)
        for i, (vsrc, vsn, nk) in enumerate(vlist):
            fw.op("pe", lambda e, i=i, nk=nk, accD=accD, nt=nt: e.matmul(
                accD, ones[0:nk, :], Pall[0:nk, i, :], start=(i == 0), stop=(i == nt - 1)),
                reads=[n_Pall, n_ones], writes=accdn2)
        fw.op("dve", lambda e, accD=accD: e.reciprocal(rden2, accD), reads=accdn2, writes=[n_rden2])
        fw.op("dve", lambda e, accO=accO: e.tensor_copy(osb, accO), reads=accn, writes=[n_osb])
        for hh in range(2):
            fw.op("dve", lambda e, hh=hh, b=b: e.tensor_tensor(
                out=catT[hh * 64:(hh + 1) * 64, 0:8, TP + b * 8:TP + b * 8 + 8],
                in0=osb[hh * 64:(hh + 1) * 64, :].rearrange("p (a c) -> p a c", c=16)[:, :, hh * 8:(hh + 1) * 8],
                in1=rden2[hh * 64:(hh + 1) * 64, :].rearrange("p (a c) -> p a c", c=16)[:, :, hh * 8:(hh + 1) * 8],
                op=ALU.mult), reads=[n_osb, n_rden2], writes=[n_catT])

    if stop == "atts":
        fw.stopped = True
    ar.seek(R_A)
    NCV = 2 + TP + TS
    hTc, n_hTc = ar.take("hTc", [16, NCV], BF16)
    wbc = [ar.take("wbc%d" % i, [16, 512], BF16) for i in range(3)]
    xtc, n_xtc = ar.take("xtc", [D], F32)
    hbc, n_hbc = ar.take("hbc", [D], BF16)
    hTt = [ar.take("hTt%d" % i, [16, 128], BF16) for i in range(2)]
    uext_p, n_uext_p = ar.take("uext_p", [TP + 2], F32)
    uext_s, n_uext_s = ar.take("uext_s", [NB, 10], F32)
    xcs, n_xcs = ar.take("xcs", [NCV], F32)
    acc_c, n_acc_c = ar.take("acc_c", [TO], F32)
    scsb, n_scsb = ar.take("scsb", [AW], F32)
    usel, n_usel = ar.take("usel", [34], F32)
    uosb, n_uosb = ar.take("uosb", [AW], F32)

    fw.dma("sp", "scsb", lambda e: e.dma_start(out=scsb[0:32], in_=sc), writes=[n_scsb])
    for ti in range(10):
        slot = ti % 2
        src = x2 if ti == 0 else xo[(ti - 1) * 128:ti * 128, :]
        norm_transpose(src, g1, n_gS, xtc, n_xtc, hbc, n_hbc, hTt[slot][0], hTt[slot][1], 2, 2 * slot)
        if ti == 0:
            fw.op("dve", lambda e, slot=slot: e.tensor_copy(hTc[:, :, 0:2], hTt[slot][0][:, :, 0:2]),
                  reads=[hTt[slot][1]], writes=[n_hTc])
        else:
            c0 = 2 + (ti - 1) * 128
            fw.op("pool", lambda e, slot=slot, c0=c0: e.tensor_copy(hTc[:, :, c0:c0 + 128], hTt[slot][0]),
                  reads=[hTt[slot][1]], writes=[n_hTc])

    chunks = [(0, 512), (512, 512), (1024, NCV - 1024)]
    pbank = 0
    for c in range(8):
        if c % 4 == 0:
            half = c // 4
            for i, cb in enumerate((2048, 1024, 0)):
                wdma("wbc%d" % i, wbc[i][0], wbc[i][1], w_in[:, 3072 + cb + half * 512:3072 + cb + (half + 1) * 512])
        cc = c % 4
        res = {}
        for wi, nm in enumerate(("xc", "C", "B")):
            wt, wn = wbc[wi]
            for (t0, tn) in chunks:
                bank = pbank % 8
                pbank += 1
                pp = psum(bank, 1)[:, 0:tn]
                pn = pnames(bank)
                for kc in range(16):
                    fw.op("pe", lambda e, kc=kc, pp=pp, t0=t0, tn=tn, wt=wt, cc=cc: e.matmul(
                        pp, wt[:, kc, cc * 128:(cc + 1) * 128], hTc[:, kc, t0:t0 + tn],
                        start=(kc == 0), stop=(kc == 15)), reads=[n_hTc, wn], writes=pn)
                res[(nm, t0)] = (pp, pn)
                if nm == "xc":
                    fw.op("act", lambda e, pp=pp, t0=t0, tn=tn: e.copy(xcs[:, t0:t0 + tn], pp),
                          reads=pn, writes=[n_xcs])
                elif nm == "C":
                    if t0 < 1024:
                        fw.op("dve", lambda e, pp=pp, t0=t0, tn=tn: e.tensor_tensor(
                            out=uext_p[:, t0:t0 + tn], in0=pp, in1=xcs[:, t0:t0 + tn], op=ALU.mult),
                            reads=pn + [n_xcs], writes=[n_uext_p])
                    else:
                        fw.op("dve", lambda e, pp=pp: e.tensor_tensor(
                            out=uext_p[:, 1024:1026], in0=pp[:, 0:2], in1=xcs[:, 1024:1026], op=ALU.mult),
                            reads=pn + [n_xcs], writes=[n_uext_p])
                        fw.op("dve", lambda e, pp=pp: e.tensor_tensor(
                            out=uext_s[:, :, 2:10], in0=pp[:, 2:130].rearrange("p (b t) -> p b t", t=8),
                            in1=xcs[:, 1026:1154].rearrange("p (b t) -> p b t", t=8), op=ALU.mult),
                            reads=pn + [n_xcs], writes=[n_uext_s])
        tb = pbank % 8
        pbank += 1
        pst = psum(tb, 1)[:, 0:32]
        fw.op("pe", lambda e, pst=pst, c=c: e.transpose(pst, scsb[0:32, c * 128:(c + 1) * 128], identf[0:32, 0:32]),
              reads=[n_scsb, n_identf], writes=pnames(tb))
        fw.op("act", lambda e, pst=pst: e.copy(uext_s[:, :, 0:2], pst.rearrange("p (b j) -> p b j", j=2)),
              reads=pnames(tb), writes=[n_uext_s])
        for (dst, srcf, un) in ((acc_c[:, 0:TP], lambda k: uext_p[:, k:k + TP], n_uext_p),
                                (acc_c[:, TP:TO].rearrange("p (b t) -> p b t", t=8),
                                 lambda k: uext_s[:, :, k:k + 8], n_uext_s)):
            fw.op("dve", lambda e, dst=dst, srcf=srcf, c=c: e.tensor_scalar(
                dst, srcf(0), cw_sb[:, c, 0:1], None, ALU.mult), reads=[un, n_cw], writes=[n_acc_c])
            for k in (1, 2):
                fw.op("dve", lambda e, dst=dst, srcf=srcf, c=c, k=k: e.scalar_tensor_tensor(
                    out=dst, in0=srcf(k), scalar=cw_sb[:, c, k:k + 1], in1=dst, op0=ALU.mult, op1=ALU.add),
                    reads=[un, n_cw, n_acc_c], writes=[n_acc_c])
        for (t0, tn) in chunks:
            pp, pn = res[("B", t0)]
            if t0 == 0:
                fw.op("dve", lambda e, pp=pp, c=c: e.tensor_tensor(
                    out=catT[:, 8 + c, 0:510], in0=pp[:, 2:512], in1=acc_c[:, 0:510], op=ALU.mult),
                    reads=pn + [n_acc_c], writes=[n_catT])
            elif t0 == 512:
                fw.op("dve", lambda e, pp=pp, c=c: e.tensor_tensor(
                    out=catT[:, 8 + c, 510:1022], in0=pp[:, 0:512], in1=acc_c[:, 510:1022], op=ALU.mult),
                    reads=pn + [n_acc_c], writes=[n_catT])
            else:
                fw.op("dve", lambda e, pp=pp, c=c: e.tensor_tensor(
                    out=catT[:, 8 + c, 1022:1152], in0=pp[:, 0:130], in1=acc_c[:, 1022:1152], op=ALU.mult),
                    reads=pn + [n_acc_c], writes=[n_catT])
        fw.op("act", lambda e: e.copy(usel[:, 0:2], uext_p[:, 1024:1026]), reads=[n_uext_p], writes=[n_usel])
        fw.op("act", lambda e: e.copy(usel[:, 2:34].rearrange("p (b j) -> p b j", j=2), uext_s[:, :, 8:10]),
              reads=[n_uext_s], writes=[n_usel])
        tb = pbank % 8
        pbank += 1
        pu = psum(tb, 1)[0:34, 0:128]
        fw.op("pe", lambda e, pu=pu: e.transpose(pu, usel, identf), reads=[n_usel, n_identf], writes=pnames(tb))
        fw.op("act", lambda e, pu=pu, c=c: e.copy(uosb[0:34, c * 128:(c + 1) * 128], pu),
              reads=pnames(tb), writes=[n_uosb])
    fw.dma("sp", "uo", lambda e: e.dma_start(out=uo_d, in_=uosb[0:34]), reads=[n_uosb], writes=["uo_out"])

    if stop == "conv":
        fw.stopped = True
    ar.seek(R_A)
    yacc, n_yacc0 = ar.take("yacc", [9, D], F32)
    h2T, n_h2T = ar.take("h2T", [16, TO], BF16)
    wob = [ar.take("wob%d" % i, [16, 512], BF16) for i in range(2)]
    xrc = [ar.take("xrc%d" % i, [512], F32) for i in range(2)]
    h2b, n_h2b = ar.take("h2b", [D], BF16)
    R_F = ar.off
    g2 = gS[:, 0:D]
    fw.dma("sp", "c_g2", lambda e: e.dma_start(out=g2, in_=g2b), writes=[n_gS])
    xi = 0
    for nch in range(4):
        wt, wn = wob[nch % 2]
        wdma("wob%d" % (nch % 2), wt, wn, w_out[:, nch * 512:(nch + 1) * 512])
        for ti in range(9):
            xr_, xn = xrc[xi % 2]
            fw.dma("sp", "xrc%d" % (xi % 2), lambda e, ti=ti, nch=nch, xr_=xr_: e.dma_start(
                out=xr_, in_=xo[ti * 128:(ti + 1) * 128, nch * 512:(nch + 1) * 512]), writes=[xn])
            bank = xi % 8
            xi += 1
            po = psum(bank, 1)
            pn = pnames(bank)
            for kc in range(16):
                fw.op("pe", lambda e, kc=kc, po=po, ti=ti, wt=wt: e.matmul(
                    po, catT[:, kc, ti * 128:(ti + 1) * 128], wt[:, kc, :], start=(kc == 0), stop=(kc == 15)),
                    reads=[n_catT, wn], writes=pn)
            fw.op("dve", lambda e, po=po, ti=ti, nch=nch, xr_=xr_: e.tensor_tensor(
                out=yacc[:, ti, nch * 512:(nch + 1) * 512], in0=po, in1=xr_, op=ALU.add),
                reads=pn + [xn], writes=[n_yacc0])
    for ti in range(9):
        ya = yacc[:, ti, :]
        st = stat[:, 12:16]
        fw.op("act", lambda e, ya=ya: e.activation(out=h2b, in_=ya, func=AF.Square, accum_out=st[:, 0:1]),
              reads=[n_yacc0], writes=[n_h2b, n_stat])
        fw.op("dve", lambda e: e.tensor_scalar(st[:, 1:2], st[:, 0:1], 1.0 / D, EPS, ALU.mult, ALU.add),
              reads=[n_stat], writes=[n_stat])
        fw.op("dve", lambda e: e.reciprocal(st[:, 2:3], st[:, 1:2]), reads=[n_stat], writes=[n_stat])
        fw.op("act", lambda e: e.activation(out=st[:, 3:4], in_=st[:, 2:3], func=AF.Sqrt),
              reads=[n_stat], writes=[n_stat])
        fw.op("dve", lambda e, ya=ya: e.scalar_tensor_tensor(out=h2b, in0=ya, scalar=st[:, 3:4], in1=g2,
                                                             op0=ALU.mult, op1=ALU.mult),
              reads=[n_yacc0, n_stat, n_gS], writes=[n_h2b])
        tb0 = 2 * (ti % 2)
        pT = psum(tb0, 2, BF16, [16, 128])
        pn = pnames(tb0, 2)
        for kc in range(16):
            fw.op("pe", lambda e, kc=kc, pT=pT: e.transpose(pT[:, kc, :], h2b[:, kc * 128:(kc + 1) * 128], ident),
                  reads=[n_h2b, n_ident], writes=pn)
        fw.op("act", lambda e, pT=pT, ti=ti: e.copy(h2T[:, 0:8, ti * 128:(ti + 1) * 128], pT[:, 0:8, :]),
              reads=[pn[0]], writes=[n_h2T])
        fw.op("dve", lambda e, pT=pT, ti=ti: e.tensor_copy(h2T[:, 8:16, ti * 128:(ti + 1) * 128], pT[:, 8:16, :]),
              reads=[pn[1]], writes=[n_h2T])

    if stop == "out":
        fw.stopped = True
    ar.seek(R_C)
    wus = [ar.take("wu%d" % i, [16, 512], BF16) for i in range(2)]
    rl = [ar.take("rl%d" % i, [512], F32) for i in range(2)]
    assert ar.off <= R_A, ar.off
    ar.seek(R_A + 108 * KB)
    wds = [ar.take("wd%d" % i, [4, D], BF16) for i in range(2)]
    hid0 = ar.take("hid0", [4, TO], BF16)
    hid1 = (gS[:, :].bitcast(BF16)[:, 0:4 * TO].rearrange("p (a b) -> p a b", a=4), n_gS)
    hid = [hid0, hid1]
    NFB = DFF // 512
    tchunks = [(0, 512), (512, 512), (1024, 128)]
    pb = 0
    ri = 0
    for fb in range(NFB):
        s = fb % 2
        wu, wun = wus[s]
        wd, wdn = wds[s]
        hd, hn = hid[s]
        if not fw.stopped:
            wdma("wu%d" % s, wu, wun, w_up[:, fb * 512:(fb + 1) * 512])
        for fc in range(4):
            fw.dma("pool", "wd%d" % s, lambda e, fb=fb, wd=wd, fc=fc: e.dma_start(
                out=wd[:, fc, :], in_=w_down[fb * 512 + fc * 128:fb * 512 + (fc + 1) * 128, :]), writes=[wdn])
        for fc in range(4):
            for (t0, tn) in tchunks:
                bank = pb % 8
                pb += 1
                pp = psum(bank, 1)[:, 0:tn]
                pn = pnames(bank)
                for kc in range(16):
                    fw.op("pe", lambda e, kc=kc, pp=pp, fc=fc, t0=t0, tn=tn, wu=wu: e.matmul(
                        pp, wu[:, kc, fc * 128:(fc + 1) * 128], h2T[:, kc, t0:t0 + tn],
                        start=(kc == 0), stop=(kc == 15)), reads=[n_h2T, wun], writes=pn)
                r, rn = rl[ri % 2][0][:, 0:tn], rl[ri % 2][1]
                ri += 1
                fw.op("act", lambda e, r=r, pp=pp: e.activation(out=r, in_=pp, func=AF.Relu), reads=pn, writes=[rn])
                fw.op("pool", lambda e, r=r, fc=fc, t0=t0, tn=tn, hd=hd: e.tensor_tensor(
                    out=hd[:, fc, t0:t0 + tn], in0=r, in1=r, op=ALU.mult), reads=[rn], writes=[hn])
        for ti in range(9):
            b0 = (pb % 2) * 4
            pb += 4
            po = psum(b0, 4)
            pns = pnames(b0, 4)
            for nch in range(4):
                for fc in range(4):
                    fw.op("pe", lambda e, nch=nch, fc=fc, po=po, ti=ti, hd=hd, wd=wd: e.matmul(
                        po[:, nch * 512:(nch + 1) * 512], hd[:, fc, ti * 128:(ti + 1) * 128],
                        wd[:, fc, nch * 512:(nch + 1) * 512], start=(fc == 0), stop=(fc == 3)),
                        reads=[hn, wdn], writes=[pns[nch]])
            ya = yacc[:, ti, :]
            for nch in range(4):
                fw.op("dve", lambda e, ya=ya, po=po, nch=nch: e.tensor_tensor(
                    out=ya[:, nch * 512:(nch + 1) * 512], in0=po[:, nch * 512:(nch + 1) * 512],
                    in1=ya[:, nch * 512:(nch + 1) * 512], op=ALU.add),
                    reads=[pns[nch], n_yacc0], writes=[n_yacc0])
            if fb == NFB - 1:
                fw.dma("sp", "yout", lambda e, ya=ya, ti=ti: e.dma_start(out=y_d[ti * 128:(ti + 1) * 128, :], in_=ya),
                       reads=[n_yacc0], writes=["y_out"])

    fw.stopped = False
    fw.final_wait("sp", ["y_out", "ko_out", "vo_out", "uo_out"])
    fw.emit()
    stack.close()
    return nc


_CACHE = {}
_STOP = None
_DBG_NB = None


def kernel(x_prompt, x_sample, state_k, state_v, state_conv,
           norm1_g, w_in, q_norm_g, k_norm_g, conv_w, w_out, norm2_g, w_up, w_down):
    f32 = np.float32
    xp = np.asarray(x_prompt, f32)[0]
    xs = np.asarray(x_sample, f32).reshape(NCORES, TS, D)
    skk = np.asarray(state_k, f32)[0].reshape(128, 2048, AW)
    svv = np.asarray(state_v, f32)[0].reshape(128, 2048, AW)
    scc = np.asarray(state_conv, f32)[0].reshape(128 * 2, AW)
    Wp, keep, Wnear, Wfar, Wnew = _tables()
    if "nc" not in _CACHE:
        _CACHE["nc"] = build_program(keep, _STOP)
    nc = _CACHE["nc"]

    w_in0 = np.ascontiguousarray(np.asarray(w_in, f32)[0])
    w_out0 = np.ascontiguousarray(np.asarray(w_out, f32)[0])
    w_up0 = np.ascontiguousarray(np.asarray(w_up, f32)[0])
    w_down0 = np.ascontiguousarray(np.asarray(w_down, f32)[0])
    g1b = np.ascontiguousarray(np.broadcast_to(np.asarray(norm1_g, f32)[0][None, :], (128, D)))
    g2b = np.ascontiguousarray(np.broadcast_to(np.asarray(norm2_g, f32)[0][None, :], (128, D)))
    qgb = np.ascontiguousarray(np.broadcast_to(np.tile(np.asarray(q_norm_g, f32)[0], NH)[None, :], (128, AW)))
    kgb = np.ascontiguousarray(np.broadcast_to(np.tile(np.asarray(k_norm_g, f32)[0], NH)[None, :], (128, AW)))
    cwT = np.ascontiguousarray(np.asarray(conv_w, f32)[0].reshape(3, 8, 128).transpose(2, 1, 0))
    ident = np.eye(128, dtype=f32)

    in_maps = []
    for c in range(NCORES):
        c0 = c * TP
        xh = np.zeros((NHALO, D), f32)
        lo = max(0, c0 - NHALO)
        if c0 > 0:
            xh[NHALO - (c0 - lo):] = xp[lo:c0]
        x2 = np.zeros((128, D), f32)
        x2[0:2] = xh[NHALO - 2:NHALO]
        xo = np.concatenate([xp[c0:c0 + TP], xs[c]], axis=0)
        tok = c0 - NHALO + np.arange(24 * 128)
        valid = np.ascontiguousarray((tok >= 0).astype(f32).reshape(24, 128).T)
        in_maps.append({
            "xh": xh, "xo": np.ascontiguousarray(xo), "x2": x2,
            "sk": skk[c * NB:(c + 1) * NB], "sv": svv[c * NB:(c + 1) * NB],
            "sc": scc[c * 2 * NB:(c + 1) * 2 * NB],
            "w_in": w_in0, "w_out": w_out0, "w_up": w_up0, "w_down": w_down0,
            "g1b": g1b, "g2b": g2b, "qgb": qgb, "kgb": kgb, "cwT": cwT, "valid": valid,
            "ident": ident, "Wp": Wp, "Wnear": Wnear, "Wfar": Wfar, "Wnew": Wnew,
        })
    res = run_bass_kernel_spmd(nc, in_maps, core_ids=list(range(NCORES)))
    R = res.results
    y = np.stack([np.asarray(r["y"], f32) for r in R])
    ko = np.stack([np.asarray(r["ko"], f32) for r in R])
    vo = np.stack([np.asarray(r["vo"], f32) for r in R])
    uo = np.stack([np.asarray(r["uo"], f32) for r in R])
    y_prompt = y[:, :TP].reshape(1, 8192, D)
    y_sample = y[:, TP:].reshape(128, 8, D)
    nkp = ko[6:8, :TP].reshape(1, 1, 2048, NH, HD)
    nvp = vo[6:8, :TP].reshape(1, 1, 2048, NH, HD)
    ncp = uo[7, 0:2].reshape(1, 1, 2, AW)
    nks = ko[:, TP:].reshape(1, 128, 8, NH, HD)
    nvs = vo[:, TP:].reshape(1, 128, 8, NH, HD)
    ncs = uo[:, 2:34].reshape(1, 128, 2, AW)
    return (np.ascontiguousarray(y_prompt), np.ascontiguousarray(y_sample), np.ascontiguousarray(nkp),
            np.ascontiguousarray(nvp), np.ascontiguousarray(ncp), np.ascontiguousarray(nks),
            np.ascontiguousarray(nvs), np.ascontiguousarray(ncs))
```

```python
import numpy as np
from contextlib import ExitStack
import concourse.bass as bass
import concourse.mybir as mybir
from concourse.bass_utils import run_bass_kernel_spmd

F32 = mybir.dt.float32
BF16 = mybir.dt.bfloat16
U8 = mybir.dt.uint8
ALU = mybir.AluOpType
AF = mybir.ActivationFunctionType
AX = mybir.AxisListType

NCORES = 8
D = 2048
DFF = 8192
NH = 16
HD = 64
AW = 1024
EPS = 1e-6
TP = 1024
TS = 128
TO = TP + TS
NHALO = 2048
NB = 16
SKIP_THRESH = 1e-30


class FW:
    ENG = ("pe", "act", "dve", "pool", "sp")

    def __init__(self, nc, stack):
        self.nc = nc
        self.stack = stack
        self.ops = {e: [] for e in self.ENG}
        self.cnt = {}
        self.sems = {}
        self.lastw = {}
        self.readers = {}
        self.iv = {}
        self.ovl = {}
        self.seen = {e: {} for e in self.ENG}
        self.stopped = False
        self.nrec = 0
        self.stop_n = None
        for e in self.ENG:
            self._sem("eng_" + e)

    def register(self, name, space, start, end):
        self.iv[name] = (space, start, end)
        self.ovl = {}

    def _over(self, name):
        o = self.ovl.get(name)
        if o is None:
            o = [name]
            if name in self.iv:
                sp, a, b = self.iv[name]
                for m, (sp2, a2, b2) in self.iv.items():
                    if m != name and sp2 == sp and a2 < b and a < b2:
                        o.append(m)
            self.ovl[name] = o
        return o

    def _sem(self, key):
        if key not in self.sems:
            self.sems[key] = self.stack.enter_context(self.nc.semaphore("s_" + key))
            self.cnt[key] = 0
        return key

    def _deps(self, eng, reads, writes, skip=None):
        ev = []
        for b in reads:
            for m in self._over(b):
                w = self.lastw.get(m)
                if w is not None:
                    ev.append(w)
        for b in writes:
            for m in self._over(b):
                w = self.lastw.get(m)
                if w is not None:
                    ev.append(w)
                ev.extend(self.readers.get(m, ()))
        waits = {}
        for (k, v) in ev:
            if (eng == "pe" and k == "eng_pe") or k == skip:
                continue
            if v > waits.get(k, 0):
                waits[k] = v
        out = []
        for k, v in waits.items():
            if self.seen[eng].get(k, 0) >= v:
                continue
            self.seen[eng][k] = v
            out.append((k, v))
        return out

    def _commit(self, event, reads, writes):
        for b in reads:
            self.readers.setdefault(b, []).append(event)
        for b in writes:
            self.lastw[b] = event
            self.readers[b] = []

    def op(self, eng, fn, reads=(), writes=()):
        self.nrec += 1
        if self.stop_n is not None and self.nrec > self.stop_n:
            self.stopped = True
        if self.stopped:
            return
        reads, writes = list(reads), list(writes)
        writes = writes + [r for r in reads if r.startswith("ps") and r not in writes]
        reads = [r for r in reads if not r.startswith("ps")]
        waits = self._deps(eng, reads, writes)
        k = "eng_" + eng
        self.cnt[k] += 1
        ev = (k, self.cnt[k])
        self.ops[eng].append((waits, fn, k, 1))
        self._commit(ev, reads, writes)

    def dma(self, eng, key, fn, reads=(), writes=()):
        self.nrec += 1
        if self.stop_n is not None and self.nrec > self.stop_n:
            self.stopped = True
        if self.stopped:
            return
        reads, writes = list(reads), list(writes)
        k = self._sem("d_" + key)
        waits = self._deps(eng, reads, writes, skip=k)
        self.cnt[k] += 16
        ev = (k, self.cnt[k])
        self.ops[eng].append((waits, fn, k, 16))
        self._commit(ev, reads, writes)

    def final_wait(self, eng, bufs):
        waits = self._deps(eng, list(bufs), ())
        self.ops[eng].append((waits, None, None, 0))

    def emit(self):
        nc = self.nc
        with nc.Block() as block:
            def mk(e):
                def body(engine):
                    for (waits, fn, k, inc) in self.ops[e]:
                        for (wk, wv) in waits:
                            engine.wait_ge(self.sems[wk], wv)
                        if fn is not None:
                            fn(engine).then_inc(self.sems[k], inc)
                return body
            block.tensor(mk("pe"))
            block.scalar(mk("act"))
            block.vector(mk("dve"))
            block.gpsimd(mk("pool"))
            block.sync(mk("sp"))


def _mult(d):
    d = np.asarray(d)
    m = ((d >= 0) & (d <= 128)).astype(np.float64)
    m += ((d >= 0) & (d <= 512) & (d % 4 == 0))
    m += ((d >= 0) & (d <= 2048) & (d % 16 == 0))
    return m


def _slopes():
    return 2.0 ** (-8.0 * np.arange(1, NH + 1, dtype=np.float64) / NH)


def _tables():
    sl = _slopes()
    b = np.arange(128)[:, None]
    a = np.arange(128)[None, :]
    Wp = np.zeros((8, 128, 17, 256), np.float32)
    for dl in range(17):
        d = 128 * dl + a - b
        m = _mult(d)
        for h in range(NH):
            w = m * np.exp(-sl[h] * np.maximum(d, 0))
            Wp[h // 2, :, dl, (h % 2) * 128:(h % 2) * 128 + 128] = w
    keep = [[dl for dl in range(17) if Wp[p, :, dl, :].max() > SKIP_THRESH] for p in range(8)]
    hh = np.repeat(np.arange(NH), 8)[None, :]
    tt = np.tile(np.arange(8), NH)[None, :]
    Wnear = np.zeros((128, 4, 128), np.float32)
    for kt in range(4):
        row = 1536 + 128 * kt + np.arange(128)[:, None]
        d = 2048 + tt - row
        Wnear[:, kt, :] = _mult(d) * np.exp(-sl[hh] * np.maximum(d, 0))
    Wfar = np.zeros((128, 8, 128), np.float32)
    for r in range(8):
        row = 16 * np.arange(96)[:, None] + r
        d = 2048 + tt - row
        w = _mult(d) * np.exp(-sl[hh] * np.maximum(d, 0))
        w = w * (tt == r)
        Wfar[:96, r, :] = w
    Wnew = np.zeros((128, NB, 128), np.float32)
    kb = (np.arange(128) // 8)[:, None]
    kt_ = (np.arange(128) % 8)[:, None]
    for bb in range(NB):
        d = tt - kt_
        Wnew[:, bb, :] = (kb == bb) * _mult(d) * np.exp(-sl[hh] * np.maximum(d, 0))
    return Wp, keep, Wnear, Wfar, Wnew


def build_program(keep, stop=None, small=False):
    nc = bass.Bass("TRN2", target_bir_lowering=False)
    stack = ExitStack()

    def din(name, shape, dt=F32):
        return nc.dram_tensor(name, list(shape), dt, kind="ExternalInput").ap()

    def dout(name, shape, dt=F32):
        return nc.dram_tensor(name, list(shape), dt, kind="ExternalOutput").ap()

    xh = din("xh", [NHALO, D])
    xo = din("xo", [TO, D])
    x2 = din("x2", [128, D])
    sk = din("sk", [NB, 2048, AW] if not small else [1, 16, AW])
    sv = din("sv", [NB, 2048, AW] if not small else [1, 16, AW])
    sc = din("sc", [2 * NB, AW])
    w_in = din("w_in", [D, 6144])
    w_out = din("w_out", [D, D])
    w_up = din("w_up", [D, DFF] if not small else [128, 128])
    w_down = din("w_down", [DFF, D] if not small else [128, 128])
    g1b = din("g1b", [128, D])
    g2b = din("g2b", [128, D])
    qgb = din("qgb", [128, AW])
    kgb = din("kgb", [128, AW])
    cwT = din("cwT", [128, 8, 3])
    valid = din("valid", [128, 24])
    identd = din("ident", [128, 128])
    Wp_d = din("Wp", [8, 128, 17, 256])
    Wnear_d = din("Wnear", [128, 4, 128])
    Wfar_d = din("Wfar", [128, 8, 128])
    Wnew_d = din("Wnew", [128, NB, 128])

    y_d = dout("y", [TO, D])
    ko_d = dout("ko", [TO, AW])
    vo_d = dout("vo", [TO, AW])
    uo_d = dout("uo", [34, AW])

    KB = 1024
    SB_BYTES = 207 * KB
    big = nc.alloc_sbuf_tensor("big", [128, SB_BYTES], U8)
    pbig = stack.enter_context(nc.psum_tensor("pbig", [128, 8 * 512], F32))
    fw = FW(nc, stack)
    if isinstance(stop, int):
        fw.stop_n = stop
    for bnk in range(8):
        fw.register("ps%d" % bnk, "psum", bnk * 2048, (bnk + 1) * 2048)

    class Arena:
        def __init__(self):
            self.off = 0
            self.uid = 0

        def seek(self, off):
            self.off = off

        def take(self, name, shape, dt):
            nb = int(np.prod(shape)) * mybir.dt.size(dt)
            nb_al = (nb + 63) // 64 * 64
            off = self.off
            self.off += nb_al
            assert off + nb_al <= SB_BYTES, ("SBUF overflow", name, off, nb_al)
            self.uid += 1
            uname = "%s#%d" % (name, self.uid)
            fw.register(uname, "sbuf", off, off + nb_al)
            v = big[:, off:off + nb].bitcast(dt)
            if len(shape) == 2:
                v = v.rearrange("p (a b) -> p a b", a=shape[0])
            elif len(shape) == 3:
                v = v.rearrange("p (a b c) -> p a b c", a=shape[0], b=shape[1])
            return v, uname

    def psum(bank, nbanks=1, dt=F32, shape=None):
        v = pbig[:, bank * 512:(bank + nbanks) * 512]
        if dt != F32:
            v = v.bitcast(dt)
        if shape is not None:
            v = v.rearrange("p (a b) -> p a b", a=shape[0])
        return v

    def wdma(key, wt, wn, src2d):
        for kc in range(16):
            fw.dma("pool", key, lambda e, kc=kc: e.dma_start(out=wt[:, kc, :], in_=src2d[kc * 128:(kc + 1) * 128, :]),
                   writes=[wn])

    def pnames(bank, n=1):
        return ["ps%d" % (bank + j) for j in range(n)]

    ar = Arena()
    R_C, R_A, R_B = 17 * KB, 53 * KB, 165 * KB
    ident, n_ident = ar.take("ident", [128], BF16)
    identf, n_identf = ar.take("identf", [128], F32)
    ones, n_ones = ar.take("ones", [128], BF16)
    valid_sb, n_valid = ar.take("valid", [24], F32)
    cw_sb, n_cw = ar.take("cw", [8, 3], F32)
    stat, n_stat = ar.take("stat", [16], F32)
    nst, n_nst = ar.take("nst", [16], F32)
    KTnew, n_KTnew = ar.take("KTnew", [8, 128], BF16)
    Vnew, n_Vnew = ar.take("Vnew", [AW], BF16)
    QTs, n_QTs = ar.take("QTs", [8, 128], BF16)
    assert ar.off <= 8 * KB, ar.off
    ar.seek(8 * KB)
    gS, n_gS = ar.take("gS", [9 * 256], F32)
    assert ar.off <= R_C
    ar.seek(R_C)
    catT, n_catT = ar.take("catT", [16, TO], BF16)
    g1 = gS[:, 0:D]

    fw.dma("pool", "c_ident", lambda e: e.dma_start(out=ident, in_=identd), writes=[n_ident])
    fw.dma("sp", "c_identf", lambda e: e.dma_start(out=identf, in_=identd), writes=[n_identf])
    fw.dma("sp", "c_valid", lambda e: e.dma_start(out=valid_sb, in_=valid), writes=[n_valid])
    fw.dma("sp", "c_cw", lambda e: e.dma_start(out=cw_sb, in_=cwT), writes=[n_cw])
    fw.dma("sp", "c_g1", lambda e: e.dma_start(out=g1, in_=g1b), writes=[n_gS])
    fw.op("dve", lambda e: e.memset(ones, 1.0), writes=[n_ones])

    def norm_transpose(src_ap, gb, n_gb, xt, n_xt, hb, n_hb, hT_dst, n_hT, slot, ps_bank):
        fw.dma("sp", "x%d" % slot, lambda e: e.dma_start(out=xt, in_=src_ap), writes=[n_xt])
        st = stat[:, 4 * slot:4 * slot + 4]
        fw.op("act", lambda e: e.activation(out=hb, in_=xt, func=AF.Square, accum_out=st[:, 0:1]),
              reads=[n_xt], writes=[n_hb, n_stat])
        fw.op("dve", lambda e: e.tensor_scalar(st[:, 1:2], st[:, 0:1], 1.0 / D, EPS, ALU.mult, ALU.add),
              reads=[n_stat], writes=[n_stat])
        fw.op("dve", lambda e: e.reciprocal(st[:, 2:3], st[:, 1:2]), reads=[n_stat], writes=[n_stat])
        fw.op("act", lambda e: e.activation(out=st[:, 3:4], in_=st[:, 2:3], func=AF.Sqrt),
              reads=[n_stat], writes=[n_stat])
        fw.op("dve", lambda e: e.scalar_tensor_tensor(out=hb, in0=xt, scalar=st[:, 3:4], in1=gb,
                                                      op0=ALU.mult, op1=ALU.mult),
              reads=[n_xt, n_stat, n_gb], writes=[n_hb])
        pT = psum(ps_bank, 2, BF16, [16, 128])
        pn = pnames(ps_bank, 2)
        for kc in range(16):
            fw.op("pe", lambda e, kc=kc: e.transpose(pT[:, kc, :], hb[:, kc * 128:(kc + 1) * 128], ident),
                  reads=[n_hb, n_ident], writes=pn)
        fw.op("act", lambda e: e.copy(hT_dst[:, 0:8, :], pT[:, 0:8, :]), reads=[pn[0]], writes=[n_hT])
        fw.op("dve", lambda e: e.tensor_copy(hT_dst[:, 8:16, :], pT[:, 8:16, :]), reads=[pn[1]], writes=[n_hT])

    ar.seek(R_A)
    KT, n_KT = ar.take("KT", [8, 3072], BF16)
    Vb, n_Vb = ar.take("Vb", [24, AW], BF16)
    QT, n_QT = ar.take("QT", [8, TP], BF16)
    assert ar.off <= R_B, ar.off
    ar.seek(R_C)
    wblk = [ar.take("wblk%d" % i, [16, 512], BF16) for i in range(2)]
    hg, n_hg = ar.take("hg", [AW], F32)
    assert ar.off <= R_A
    ar.seek(R_B)
    xts = [ar.take("xt%d" % i, [D], F32) for i in range(2)]
    hbs = [ar.take("hb%d" % i, [D], BF16) for i in range(2)]
    hTs = [ar.take("hT%d" % i, [16, 128], BF16) for i in range(2)]
    tmpf, n_tmpf = ar.take("tmpf", [AW], F32)
    kst, n_kst = ar.take("kst", [AW], F32)
    knb, n_knb = ar.take("knb", [AW], BF16)
    assert ar.off <= SB_BYTES

    def head_norm(ps_ap, psn, out_f32, out_bf, scale):
        ns = nst[:, 0:16]
        for hf in range(2):
            fw.op("act", lambda e, hf=hf: e.activation(out=tmpf[:, hf * 512:(hf + 1) * 512],
                                                       in_=ps_ap[:, hf * 512:(hf + 1) * 512], func=AF.Square),
                  reads=[psn[hf]], writes=[n_tmpf])
        fw.op("dve", lambda e: e.tensor_reduce(out=ns, in_=tmpf.rearrange("p (h d) -> p h d", d=HD),
                                               axis=AX.X, op=ALU.add), reads=[n_tmpf], writes=[n_nst])
        fw.op("dve", lambda e: e.tensor_scalar(ns, ns, 1.0 / HD, EPS, ALU.mult, ALU.add), reads=[n_nst], writes=[n_nst])
        fw.op("dve", lambda e: e.reciprocal(ns, ns), reads=[n_nst], writes=[n_nst])
        fw.op("act", lambda e: e.activation(out=ns, in_=ns, func=AF.Sqrt), reads=[n_nst], writes=[n_nst])
        for hf in range(2):
            fw.op("dve", lambda e, hf=hf: e.tensor_tensor(
                out=tmpf[:, hf * 512:(hf + 1) * 512].rearrange("p (h d) -> p h d", d=HD),
                in0=ps_ap[:, hf * 512:(hf + 1) * 512].rearrange("p (h d) -> p h d", d=HD),
                in1=ns[:, hf * 8:(hf + 1) * 8].unsqueeze(2).to_broadcast([128, 8, HD]), op=ALU.mult),
                reads=[psn[hf], n_nst], writes=[n_tmpf])
        fw.op("dve", lambda e: e.tensor_tensor(out=out_f32, in0=tmpf, in1=hg, op=ALU.mult),
              reads=[n_tmpf, n_hg], writes=[n_kst])
        fw.op("act", lambda e: e.activation(out=out_bf, in_=out_f32, func=AF.Copy, scale=scale),
              reads=[n_kst], writes=[n_knb])

    tcount = [0]

    def transpose8(dst_ap, n_dst, bank):
        pT = psum(bank, 1, BF16, [8, 128])
        pn = pnames(bank)
        for p in range(8):
            fw.op("pe", lambda e, p=p: e.transpose(pT[:, p, :], knb[:, p * 128:(p + 1) * 128], ident),
                  reads=[n_knb, n_ident], writes=pn)
        fw.op("act", lambda e: e.copy(dst_ap, pT), reads=pn, writes=[n_dst])

    for (pname, wcol, tiles) in (("k", 1024, range(25)), ("v", 2048, range(25)), ("q", 0, range(16, 25))):
        for i in range(2):
            wdma("wblk%d" % i, wblk[i][0], wblk[i][1], w_in[:, wcol + i * 512:wcol + (i + 1) * 512])
        if pname == "k":
            fw.dma("sp", "hg", lambda e: e.dma_start(out=hg, in_=kgb), writes=[n_hg])
        elif pname == "q":
            fw.dma("sp", "hg", lambda e: e.dma_start(out=hg, in_=qgb), writes=[n_hg])
        def emit_nt(tj):
            sl = tj % 2
            srcj = xh[tj * 128:(tj + 1) * 128, :] if tj < 16 else xo[(tj - 16) * 128:(tj - 15) * 128, :]
            norm_transpose(srcj, g1, n_gS, xts[sl][0], xts[sl][1], hbs[sl][0], hbs[sl][1],
                           hTs[sl][0], hTs[sl][1], sl, 2 * sl)

        tlist = list(tiles)
        emit_nt(tlist[0])
        for tidx, ti in enumerate(tlist):
            if tidx + 1 < len(tlist):
                emit_nt(tlist[tidx + 1])
            slot = ti % 2
            own = ti >= 16
            r0 = (ti - 16) * 128
            hT, n_hT = hTs[slot]
            pkb = 4 + 2 * (ti % 2)
            pk = psum(pkb, 2)
            pkn = pnames(pkb, 2)
            for nch in range(2):
                for kc in range(16):
                    fw.op("pe", lambda e, nch=nch, kc=kc, hT=hT, pk=pk: e.matmul(
                        pk[:, nch * 512:(nch + 1) * 512], hT[:, kc, :], wblk[nch][0][:, kc, :],
                        start=(kc == 0), stop=(kc == 15)),
                        reads=[n_hT, wblk[nch][1]], writes=[pkn[nch]])
            if pname == "v":
                vdst, n_vdst = (Vb[:, ti, :], n_Vb) if ti < 24 else (Vnew, n_Vnew)
                for hf in range(2):
                    fw.op("dve", lambda e, vdst=vdst, hf=hf, pk=pk: e.tensor_copy(
                        vdst[:, hf * 512:(hf + 1) * 512], pk[:, hf * 512:(hf + 1) * 512]),
                        reads=[pkn[hf]], writes=[n_vdst])
                if own:
                    for hf in range(2):
                        fw.op("act", lambda e, hf=hf, pk=pk: e.copy(kst[:, hf * 512:(hf + 1) * 512],
                                                             pk[:, hf * 512:(hf + 1) * 512]),
                              reads=[pkn[hf]], writes=[n_kst])
                    fw.dma("sp", "vo", lambda e, r0=r0: e.dma_start(out=vo_d[r0:r0 + 128, :], in_=kst),
                           reads=[n_kst], writes=["vo_out"])
            elif pname == "k":
                head_norm(pk, pkn, kst, knb, 1.0)
                if own:
                    fw.dma("sp", "ko", lambda e, r0=r0: e.dma_start(out=ko_d[r0:r0 + 128, :], in_=kst),
                           reads=[n_kst], writes=["ko_out"])
                if ti < 24:
                    transpose8(KT[:, :, ti * 128:(ti + 1) * 128], n_KT, pkb)
                else:
                    transpose8(KTnew, n_KTnew, pkb)
            else:
                head_norm(pk, pkn, kst, knb, HD ** -0.5)
                if ti < 24:
                    transpose8(QT[:, :, r0:r0 + 128], n_QT, pkb)
                else:
                    transpose8(QTs, n_QTs, pkb)

    if stop == "kv":
        fw.stopped = True
    ar.seek(R_B)
    Wps = [ar.take("Wp%d" % i, [17, 256], BF16) for i in range(2)]
    Es = [ar.take("E%d" % i, [256], F32) for i in range(3)]
    Ps = [ar.take("P%d" % i, [256], BF16) for i in range(3)]
    rden, n_rden = ar.take("rden", [256], F32)
    vones, n_vones = ar.take("vones", [16, 128], BF16)
    fw.op("dve", lambda e: e.tensor_copy(vones, valid_sb[:, 0:16].unsqueeze(2).to_broadcast([128, 16, 128])),
          reads=[n_valid], writes=[n_vones])
    QTbd = [ar.take("QTbd%d" % i, [8, 256], BF16) for i in range(2)]
    for i in range(2):
        fw.op("dve", lambda e, i=i: e.memset(QTbd[i][0], 0.0), writes=[QTbd[i][1]])
    tl = []
    for p in range(8):
        dls = keep[p]
        for qb in range(8):
            for i, dl in enumerate(dls):
                tl.append(dict(p=p, qb=qb, i=i, dl=dl, n=len(dls), kt=16 + qb - dl, idx=len(tl)))
    DEPTH = 2
    NR = DEPTH + 1
    Es = Es[:NR]
    Ps = Ps[:NR]

    def front(t):
        p, qb, dl, kt, j = t["p"], t["qb"], t["dl"], t["kt"], t["idx"]
        Qb, Qbn = QTbd[p % 2]
        Wt, Wn = Wps[p % 2]
        if qb == 0 and t["i"] == 0:
            for hh in range(2):
                fw.op("dve", lambda e, hh=hh, p=p, Qb=Qb: e.tensor_copy(
                    Qb[hh * 64:(hh + 1) * 64, :, hh * 128:(hh + 1) * 128],
                    QT[hh * 64:(hh + 1) * 64, p, :].rearrange("p (q a) -> p q a", a=128)),
                    reads=[n_QT], writes=[Qbn])
            fw.dma("pool", "Wp%d" % (p % 2), lambda e, p=p, Wt=Wt: e.dma_start(out=Wt, in_=Wp_d[p]), writes=[Wn])
        kcol = kt * 128
        sbank = 2 + (j % NR)
        S = psum(sbank, 1)[:, 0:256]
        Sn = pnames(sbank)
        E, En = Es[j % NR]
        P, Pn = Ps[j % NR]
        fw.op("pe", lambda e, S=S, kcol=kcol, qb=qb, p=p, Qb=Qb: e.matmul(
            S, KT[:, p, kcol:kcol + 128], Qb[:, qb, :], start=True, stop=True),
            reads=[n_KT, Qbn], writes=Sn)
        fw.op("act", lambda e, E=E, S=S: e.activation(out=E, in_=S, func=AF.Exp), reads=Sn, writes=[En])
        fw.op("dve", lambda e, P=P, E=E, dl=dl, Wt=Wt: e.tensor_tensor(
            out=P, in0=E, in1=Wt[:, dl, :], op=ALU.mult), reads=[En, Wn], writes=[Pn])

    def back(t):
        p, qb, kt, j = t["p"], t["qb"], t["kt"], t["idx"]
        P, Pn = Ps[j % NR]
        accb = qb % 2
        acc = psum(accb, 1)
        accd = psum(5 + accb, 1)
        accn = pnames(accb)
        accdn = pnames(5 + accb)
        qcol = qb * 128
        first = (t["i"] == 0)
        last = (t["i"] == t["n"] - 1)
        fw.op("pe", lambda e, P=P, kt=kt, p=p, acc=acc, first=first, last=last: e.matmul(
            acc[:, 0:256], Vb[:, kt, p * 128:(p + 1) * 128], P, start=first, stop=last),
            reads=[Pn, n_Vb], writes=accn)
        onesv = vones[:, kt, :] if kt < 16 else ones
        fw.op("pe", lambda e, P=P, accd=accd, first=first, last=last, onesv=onesv: e.matmul(
            accd[:, 0:256], onesv, P, start=first, stop=last),
            reads=[Pn, n_ones, n_vones], writes=accdn)
        if last:
            fw.op("dve", lambda e, accd=accd: e.reciprocal(rden, accd[:, 0:256]), reads=accdn, writes=[n_rden])
            fw.op("dve", lambda e, acc=acc, p=p, qcol=qcol: e.tensor_tensor(
                out=catT[0:64, p, qcol:qcol + 128], in0=acc[0:64, 0:128], in1=rden[0:64, 0:128], op=ALU.mult),
                reads=accn + [n_rden], writes=[n_catT])
            fw.op("dve", lambda e, acc=acc, p=p, qcol=qcol: e.tensor_tensor(
                out=catT[64:128, p, qcol:qcol + 128], in0=acc[64:128, 128:256], in1=rden[64:128, 128:256],
                op=ALU.mult), reads=accn + [n_rden], writes=[n_catT])

    for j in range(len(tl) + DEPTH):
        if j < len(tl):
            front(tl[j])
        if j - DEPTH >= 0:
            back(tl[j - DEPTH])

    if stop == "attp":
        fw.stopped = True
    ar.seek(R_A)
    Qbd, n_Qbd = ar.take("Qbd", [8, NB, 16], BF16)
    Wnear, n_Wnear = ar.take("Wnear", [4, 128], BF16)
    Wfar, n_Wfar = ar.take("Wfar", [8, 128], BF16)
    Wnew, n_Wnew = ar.take("Wnew", [NB, 128], BF16)
    kfar = [ar.take("kfar%d" % i, [8, AW], BF16) for i in range(2)]
    knear = [ar.take("knear%d" % i, [4, AW], BF16) for i in range(2)]
    vfar = [ar.take("vfar%d" % i, [8, AW], BF16) for i in range(2)]
    vnear = [ar.take("vnear%d" % i, [4, AW], BF16) for i in range(2)]
    KTs = [ar.take("KTs%d" % i, [8, 128], BF16) for i in range(3)]
    Es2 = [ar.take("E2_%d" % i, [128], F32) for i in range(3)]
    Pall, n_Pall = ar.take("Pall", [13, 128], BF16)
    rden2, n_rden2 = ar.take("rden2", [128], F32)
    osb, n_osb = ar.take("osb", [128], F32)

    fw.dma("pool", "Wnear", lambda e: e.dma_start(out=Wnear, in_=Wnear_d), writes=[n_Wnear])
    fw.dma("pool", "Wfar", lambda e: e.dma_start(out=Wfar, in_=Wfar_d), writes=[n_Wfar])
    fw.dma("pool", "Wnew", lambda e: e.dma_start(out=Wnew, in_=Wnew_d), writes=[n_Wnew])
    fw.op("dve", lambda e: e.memset(Qbd, 0.0), writes=[n_Qbd])
    for hh in range(2):
        fw.op("dve", lambda e, hh=hh: e.tensor_copy(
            Qbd[hh * 64:(hh + 1) * 64, :, :, hh * 8:(hh + 1) * 8],
            QTs[hh * 64:(hh + 1) * 64, :, :].rearrange("p a (b t) -> p a b t", t=8)),
            reads=[n_QTs], writes=[n_Qbd])

    it = 0
    for b in range(NB if _DBG_NB is None else _DBG_NB):
        s = b % 2
        for (far_t, near_t, srcd, nm) in ((kfar[s], knear[s], sk, "k"), (vfar[s], vnear[s], sv, "v")):
            for r in range(8):
                fw.dma("pool", "%sfar%d" % (nm, s), lambda e, b=b, r=r, far_t=far_t, srcd=srcd: e.dma_start(
                    out=far_t[0][0:96, r, :],
                    in_=srcd[b, 0:1536, :].rearrange("(j r) c -> j r c", r=16)[:, r, :]), writes=[far_t[1]])
            for t in range(4):
                fw.dma("pool", "%snear%d" % (nm, s), lambda e, b=b, t=t, near_t=near_t, srcd=srcd: e.dma_start(
                    out=near_t[0][:, t, :], in_=srcd[b, 1536 + t * 128:1536 + (t + 1) * 128, :]), writes=[near_t[1]])
        accb = b % 2
        accO = psum(accb, 1)[:, 0:128]
        accD = psum(7, 1)[:, 0:128]
        accn = pnames(accb)
        accdn2 = pnames(7)
        vlist = []
        tiles = [("far", r) for r in range(8)] + [("near", k) for k in range(4)] + [("new", 0)]
        for i, (kind, j) in enumerate(tiles):
            nk = 96 if kind == "far" else 128
            if kind == "far":
                ksrc, ksn = kfar[s][0][0:96, j, :], kfar[s][1]
                vsrc, vsn = vfar[s][0][0:96, j, :], vfar[s][1]
                W, Wn = Wfar[0:96, j, :], n_Wfar
            elif kind == "near":
                ksrc, ksn = knear[s][0][:, j, :], knear[s][1]
                vsrc, vsn = vnear[s][0][:, j, :], vnear[s][1]
                W, Wn = Wnear[:, j, :], n_Wnear
            else:
                ksrc, ksn = None, None
                vsrc, vsn = Vnew, n_Vnew
                W, Wn = Wnew[:, b, :], n_Wnew
            r3 = it % 3
            it += 1
            if kind != "new":
                KTt, KTn = KTs[r3]
                tb = 5 + (it % 2)
                pT = psum(tb, 1, BF16, [8, 128])
                for p in range(8):
                    fw.op("pe", lambda e, p=p, pT=pT, ksrc=ksrc, nk=nk: e.transpose(
                        pT[:, p, 0:nk], ksrc[:, p * 128:(p + 1) * 128], ident[0:nk, 0:nk]),
                        reads=[ksn, n_ident], writes=pnames(tb))
                fw.op("dve", lambda e, KTt=KTt, pT=pT, nk=nk: e.tensor_copy(KTt[:, :, 0:nk], pT[:, :, 0:nk]),
                      reads=pnames(tb), writes=[KTn])
                ktv = lambda p, KTt=KTt, nk=nk: KTt[:, p, 0:nk]
            else:
                KTn = n_KTnew
                ktv = lambda p: KTnew[:, p, :]
            sbank = 2 + r3
            S = psum(sbank, 1)[0:nk, 0:128]
            Sn = pnames(sbank)
            for p in range(8):
                fw.op("pe", lambda e, p=p, S=S, ktv=ktv, b=b: e.matmul(
                    S[:, p * 16:(p + 1) * 16], ktv(p), Qbd[:, p, b, :], start=True, stop=True),
                    reads=[KTn, n_Qbd], writes=Sn)
            E, En = Es2[r3][0][0:nk], Es2[r3][1]
            P = Pall[0:nk, i, :]
            fw.op("act", lambda e, E=E, S=S: e.activation(out=E, in_=S, func=AF.Exp), reads=Sn, writes=[En])
            fw.op("dve", lambda e, P=P, E=E, W=W: e.tensor_tensor(out=P, in0=E, in1=W, op=ALU.mult),
                  reads=[En, Wn], writes=[n_Pall])
            vlist.append((vsrc, vsn, nk))
        nt = len(tiles)
        for p in range(8):
            for i, (vsrc, vsn, nk) in enumerate(vlist):
                fw.op("pe", lambda e, p=p, i=i, vsrc=vsrc, nk=nk, accO=accO: e.matmul(
                    accO[:, p * 16:(p + 1) * 16], vsrc[:, p * 128:(p + 1) * 128], Pall[0:nk, i, p * 16:(p + 1) * 16],
                    start=(i == 0), stop=(i == nt - 1)), reads=[n_Pall, vsn], writes=accn)
        for i, (vsrc, vsn, nk) in enumerate(vlist):
            fw.op("pe", lambda e, i=i, nk=nk, accD=accD, nt=nt: e.matmul(
                accD, ones[0:nk, :], Pall[0:nk, i, :], start=(i == 0), stop=(i == nt - 1)),
                reads=[n_Pall, n_ones], writes=accdn2)
        fw.op("dve", lambda e, accD=accD: e.reciprocal(rden2, accD), reads=accdn2, writes=[n_rden2])
        fw.op("dve", lambda e, accO=accO: e.tensor_copy(osb, accO), reads=accn, writes=[n_osb])
        for hh in range(2):
            fw.op("dve", lambda e, hh=hh, b=b: e.tensor_tensor(
                out=catT[hh * 64:(hh + 1) * 64, 0:8, TP + b * 8:TP + b * 8 + 8],
                in0=osb[hh * 64:(hh + 1) * 64, :].rearrange("p (a c) -> p a c", c=16)[:, :, hh * 8:(hh + 1) * 8],
                in1=rden2[hh * 64:(hh + 1) * 64, :].rearrange("p (a c) -> p a c", c=16)[:, :, hh * 8:(hh + 1) * 8],
                op=ALU.mult), reads=[n_osb, n_rden2], writes=[n_catT])

    if stop == "atts":
        fw.stopped = True
    ar.seek(R_A)
    NCV = 2 + TP + TS
    hTc, n_hTc = ar.take("hTc", [16, NCV], BF16)
    wbc = [ar.take("wbc%d" % i, [16, 512], BF16) for i in range(3)]
    xtc, n_xtc = ar.take("xtc", [D], F32)
    hbc, n_hbc = ar.take("hbc", [D], BF16)
    hTt = [ar.take("hTt%d" % i, [16, 128], BF16) for i in range(2)]
    uext_p, n_uext_p = ar.take("uext_p", [TP + 2], F32)
    uext_s, n_uext_s = ar.take("uext_s", [NB, 10], F32)
    xcs, n_xcs = ar.take("xcs", [NCV], F32)
    acc_c, n_acc_c = ar.take("acc_c", [TO], F32)
    scsb, n_scsb = ar.take("scsb", [AW], F32)
    usel, n_usel = ar.take("usel", [34], F32)
    uosb, n_uosb = ar.take("uosb", [AW], F32)

    fw.dma("sp", "scsb", lambda e: e.dma_start(out=scsb[0:32], in_=sc), writes=[n_scsb])
    for ti in range(10):
        slot = ti % 2
        src = x2 if ti == 0 else xo[(ti - 1) * 128:ti * 128, :]
        norm_transpose(src, g1, n_gS, xtc, n_xtc, hbc, n_hbc, hTt[slot][0], hTt[slot][1], 2, 2 * slot)
        if ti == 0:
            fw.op("dve", lambda e, slot=slot: e.tensor_copy(hTc[:, :, 0:2], hTt[slot][0][:, :, 0:2]),
                  reads=[hTt[slot][1]], writes=[n_hTc])
        else:
            c0 = 2 + (ti - 1) * 128
            fw.op("pool", lambda e, slot=slot, c0=c0: e.tensor_copy(hTc[:, :, c0:c0 + 128], hTt[slot][0]),
                  reads=[hTt[slot][1]], writes=[n_hTc])

    chunks = [(0, 512), (512, 512), (1024, NCV - 1024)]
    pbank = 0
    for c in range(8):
        if c % 4 == 0:
            half = c // 4
            for i, cb in enumerate((2048, 1024, 0)):
                wdma("wbc%d" % i, wbc[i][0], wbc[i][1], w_in[:, 3072 + cb + half * 512:3072 + cb + (half + 1) * 512])
        cc = c % 4
        res = {}
        for wi, nm in enumerate(("xc", "C", "B")):
            wt, wn = wbc[wi]
            for (t0, tn) in chunks:
                bank = pbank % 8
                pbank += 1
                pp = psum(bank, 1)[:, 0:tn]
                pn = pnames(bank)
                for kc in range(16):
                    fw.op("pe", lambda e, kc=kc, pp=pp, t0=t0, tn=tn, wt=wt, cc=cc: e.matmul(
                        pp, wt[:, kc, cc * 128:(cc + 1) * 128], hTc[:, kc, t0:t0 + tn],
                        start=(kc == 0), stop=(kc == 15)), reads=[n_hTc, wn], writes=pn)
                res[(nm, t0)] = (pp, pn)
                if nm == "xc":
                    fw.op("act", lambda e, pp=pp, t0=t0, tn=tn: e.copy(xcs[:, t0:t0 + tn], pp),
                          reads=pn, writes=[n_xcs])
                elif nm == "C":
                    if t0 < 1024:
                        fw.op("dve", lambda e, pp=pp, t0=t0, tn=tn: e.tensor_tensor(
                            out=uext_p[:, t0:t0 + tn], in0=pp, in1=xcs[:, t0:t0 + tn], op=ALU.mult),
                            reads=pn + [n_xcs], writes=[n_uext_p])
                    else:
                        fw.op("dve", lambda e, pp=pp: e.tensor_tensor(
                            out=uext_p[:, 1024:1026], in0=pp[:, 0:2], in1=xcs[:, 1024:1026], op=ALU.mult),
                            reads=pn + [n_xcs], writes=[n_uext_p])
                        fw.op("dve", lambda e, pp=pp: e.tensor_tensor(
                            out=uext_s[:, :, 2:10], in0=pp[:, 2:130].rearrange("p (b t) -> p b t", t=8),
                            in1=xcs[:, 1026:1154].rearrange("p (b t) -> p b t", t=8), op=ALU.mult),
                            reads=pn + [n_xcs], writes=[n_uext_s])
        tb = pbank % 8
        pbank += 1
        pst = psum(tb, 1)[:, 0:32]
        fw.op("pe", lambda e, pst=pst, c=c: e.transpose(pst, scsb[0:32, c * 128:(c + 1) * 128], identf[0:32, 0:32]),
              reads=[n_scsb, n_identf], writes=pnames(tb))
        fw.op("act", lambda e, pst=pst: e.copy(uext_s[:, :, 0:2], pst.rearrange("p (b j) -> p b j", j=2)),
              reads=pnames(tb), writes=[n_uext_s])
        for (dst, srcf, un) in ((acc_c[:, 0:TP], lambda k: uext_p[:, k:k + TP], n_uext_p),
                                (acc_c[:, TP:TO].rearrange("p (b t) -> p b t", t=8),
                                 lambda k: uext_s[:, :, k:k + 8], n_uext_s)):
            fw.op("dve", lambda e, dst=dst, srcf=srcf, c=c: e.tensor_scalar(
                dst, srcf(0), cw_sb[:, c, 0:1], None, ALU.mult), reads=[un, n_cw], writes=[n_acc_c])
            for k in (1, 2):
                fw.op("dve", lambda e, dst=dst, srcf=srcf, c=c, k=k: e.scalar_tensor_tensor(
                    out=dst, in0=srcf(k), scalar=cw_sb[:, c, k:k + 1], in1=dst, op0=ALU.mult, op1=ALU.add),
                    reads=[un, n_cw, n_acc_c], writes=[n_acc_c])
        for (t0, tn) in chunks:
            pp, pn = res[("B", t0)]
            if t0 == 0:
                fw.op("dve", lambda e, pp=pp, c=c: e.tensor_tensor(
                    out=catT[:, 8 + c, 0:510], in0=pp[:, 2:512], in1=acc_c[:, 0:510], op=ALU.mult),
                    reads=pn + [n_acc_c], writes=[n_catT])
            elif t0 == 512:
                fw.op("dve", lambda e, pp=pp, c=c: e.tensor_tensor(
                    out=catT[:, 8 + c, 510:1022], in0=pp[:, 0:512], in1=acc_c[:, 510:1022], op=ALU.mult),
                    reads=pn + [n_acc_c], writes=[n_catT])
            else:
                fw.op("dve", lambda e, pp=pp, c=c: e.tensor_tensor(
                    out=catT[:, 8 + c, 1022:1152], in0=pp[:, 0:130], in1=acc_c[:, 1022:1152], op=ALU.mult),
                    reads=pn + [n_acc_c], writes=[n_catT])
        fw.op("act", lambda e: e.copy(usel[:, 0:2], uext_p[:, 1024:1026]), reads=[n_uext_p], writes=[n_usel])
        fw.op("act", lambda e: e.copy(usel[:, 2:34].rearrange("p (b j) -> p b j", j=2), uext_s[:, :, 8:10]),
              reads=[n_uext_s], writes=[n_usel])
        tb = pbank % 8
        pbank += 1
        pu = psum(tb, 1)[0:34, 0:128]
        fw.op("pe", lambda e, pu=pu: e.transpose(pu, usel, identf), reads=[n_usel, n_identf], writes=pnames(tb))
        fw.op("act", lambda e, pu=pu, c=c: e.copy(uosb[0:34, c * 128:(c + 1) * 128], pu),
              reads=pnames(tb), writes=[n_uosb])
    fw.dma("sp", "uo", lambda e: e.dma_start(out=uo_d, in_=uosb[0:34]), reads=[n_uosb], writes=["uo_out"])

    if stop == "conv":
        fw.stopped = True
    ar.seek(R_A)
    yacc, n_yacc0 = ar.take("yacc", [9, D], F32)
    h2T, n_h2T = ar.take("h2T", [16, TO], BF16)
    wob = [ar.take("wob%d" % i, [16, 512], BF16) for i in range(2)]
    xrc = [ar.take("xrc%d" % i, [512], F32) for i in range(2)]
    h2b, n_h2b = ar.take("h2b", [D], BF16)
    R_F = ar.off
    g2 = gS[:, 0:D]
    fw.dma("sp", "c_g2", lambda e: e.dma_start(out=g2, in_=g2b), writes=[n_gS])
    xi = 0
    for nch in range(4):
        wt, wn = wob[nch % 2]
        wdma("wob%d" % (nch % 2), wt, wn, w_out[:, nch * 512:(nch + 1) * 512])
        for ti in range(9):
            xr_, xn = xrc[xi % 2]
            fw.dma("sp", "xrc%d" % (xi % 2), lambda e, ti=ti, nch=nch, xr_=xr_: e.dma_start(
                out=xr_, in_=xo[ti * 128:(ti + 1) * 128, nch * 512:(nch + 1) * 512]), writes=[xn])
            bank = xi % 8
            xi += 1
            po = psum(bank, 1)
            pn = pnames(bank)
            for kc in range(16):
                fw.op("pe", lambda e, kc=kc, po=po, ti=ti, wt=wt: e.matmul(
                    po, catT[:, kc, ti * 128:(ti + 1) * 128], wt[:, kc, :], start=(kc == 0), stop=(kc == 15)),
                    reads=[n_catT, wn], writes=pn)
            fw.op("dve", lambda e, po=po, ti=ti, nch=nch, xr_=xr_: e.tensor_tensor(
                out=yacc[:, ti, nch * 512:(nch + 1) * 512], in0=po, in1=xr_, op=ALU.add),
                reads=pn + [xn], writes=[n_yacc0])
    for ti in range(9):
        ya = yacc[:, ti, :]
        st = stat[:, 12:16]
        fw.op("act", lambda e, ya=ya: e.activation(out=h2b, in_=ya, func=AF.Square, accum_out=st[:, 0:1]),
              reads=[n_yacc0], writes=[n_h2b, n_stat])
        fw.op("dve", lambda e: e.tensor_scalar(st[:, 1:2], st[:, 0:1], 1.0 / D, EPS, ALU.mult, ALU.add),
              reads=[n_stat], writes=[n_stat])
        fw.op("dve", lambda e: e.reciprocal(st[:, 2:3], st[:, 1:2]), reads=[n_stat], writes=[n_stat])
        fw.op("act", lambda e: e.activation(out=st[:, 3:4], in_=st[:, 2:3], func=AF.Sqrt),
              reads=[n_stat], writes=[n_stat])
        fw.op("dve", lambda e, ya=ya: e.scalar_tensor_tensor(out=h2b, in0=ya, scalar=st[:, 3:4], in1=g2,
                                                             op0=ALU.mult, op1=ALU.mult),
              reads=[n_yacc0, n_stat, n_gS], writes=[n_h2b])
        tb0 = 2 * (ti % 2)
        pT = psum(tb0, 2, BF16, [16, 128])
        pn = pnames(tb0, 2)
        for kc in range(16):
            fw.op("pe", lambda e, kc=kc, pT=pT: e.transpose(pT[:, kc, :], h2b[:, kc * 128:(kc + 1) * 128], ident),
                  reads=[n_h2b, n_ident], writes=pn)
        fw.op("act", lambda e, pT=pT, ti=ti: e.copy(h2T[:, 0:8, ti * 128:(ti + 1) * 128], pT[:, 0:8, :]),
              reads=[pn[0]], writes=[n_h2T])
        fw.op("dve", lambda e, pT=pT, ti=ti: e.tensor_copy(h2T[:, 8:16, ti * 128:(ti + 1) * 128], pT[:, 8:16, :]),
              reads=[pn[1]], writes=[n_h2T])

    if stop == "out":
        fw.stopped = True
    ar.seek(R_C)
    wus = [ar.take("wu%d" % i, [16, 512], BF16) for i in range(2)]
    rl = [ar.take("rl%d" % i, [512], F32) for i in range(2)]
    assert ar.off <= R_A, ar.off
    ar.seek(R_A + 108 * KB)
    wds = [ar.take("wd%d" % i, [4, D], BF16) for i in range(2)]
    hid0 = ar.take("hid0", [4, TO], BF16)
    hid1 = (gS[:, :].bitcast(BF16)[:, 0:4 * TO].rearrange("p (a b) -> p a b", a=4), n_gS)
    hid = [hid0, hid1]
    NFB = DFF // 512
    tchunks = [(0, 512), (512, 512), (1024, 128)]
    pb = 0
    ri = 0
    for fb in range(NFB):
        s = fb % 2
        wu, wun = wus[s]
        wd, wdn = wds[s]
        hd, hn = hid[s]
        if not fw.stopped:
            wdma("wu%d" % s, wu, wun, w_up[:, fb * 512:(fb + 1) * 512])
        for fc in range(4):
            fw.dma("pool", "wd%d" % s, lambda e, fb=fb, wd=wd, fc=fc: e.dma_start(
                out=wd[:, fc, :], in_=w_down[fb * 512 + fc * 128:fb * 512 + (fc + 1) * 128, :]), writes=[wdn])
        for fc in range(4):
            for (t0, tn) in tchunks:
                bank = pb % 8
                pb += 1
                pp = psum(bank, 1)[:, 0:tn]
                pn = pnames(bank)
                for kc in range(16):
                    fw.op("pe", lambda e, kc=kc, pp=pp, fc=fc, t0=t0, tn=tn, wu=wu: e.matmul(
                        pp, wu[:, kc, fc * 128:(fc + 1) * 128], h2T[:, kc, t0:t0 + tn],
                        start=(kc == 0), stop=(kc == 15)), reads=[n_h2T, wun], writes=pn)
                r, rn = rl[ri % 2][0][:, 0:tn], rl[ri % 2][1]
                ri += 1
                fw.op("act", lambda e, r=r, pp=pp: e.activation(out=r, in_=pp, func=AF.Relu), reads=pn, writes=[rn])
                fw.op("pool", lambda e, r=r, fc=fc, t0=t0, tn=tn, hd=hd: e.tensor_tensor(
                    out=hd[:, fc, t0:t0 + tn], in0=r, in1=r, op=ALU.mult), reads=[rn], writes=[hn])
        for ti in range(9):
            b0 = (pb % 2) * 4
            pb += 4
            po = psum(b0, 4)
            pns = pnames(b0, 4)
            for nch in range(4):
                for fc in range(4):
                    fw.op("pe", lambda e, nch=nch, fc=fc, po=po, ti=ti, hd=hd, wd=wd: e.matmul(
                        po[:, nch * 512:(nch + 1) * 512], hd[:, fc, ti * 128:(ti + 1) * 128],
                        wd[:, fc, nch * 512:(nch + 1) * 512], start=(fc == 0), stop=(fc == 3)),
                        reads=[hn, wdn], writes=[pns[nch]])
            ya = yacc[:, ti, :]
            for nch in range(4):
                fw.op("dve", lambda e, ya=ya, po=po, nch=nch: e.tensor_tensor(
                    out=ya[:, nch * 512:(nch + 1) * 512], in0=po[:, nch * 512:(nch + 1) * 512],
                    in1=ya[:, nch * 512:(nch + 1) * 512], op=ALU.add),
                    reads=[pns[nch], n_yacc0], writes=[n_yacc0])
            if fb == NFB - 1:
                fw.dma("sp", "yout", lambda e, ya=ya, ti=ti: e.dma_start(out=y_d[ti * 128:(ti + 1) * 128, :], in_=ya),
                       reads=[n_yacc0], writes=["y_out"])

    fw.stopped = False
    fw.final_wait("sp", ["y_out", "ko_out", "vo_out", "uo_out"])
    fw.emit()
    stack.close()
    return nc


_CACHE = {}
_STOP = None
_DBG_NB = None


def kernel(x_prompt, x_sample, state_k, state_v, state_conv,
           norm1_g, w_in, q_norm_g, k_norm_g, conv_w, w_out, norm2_g, w_up, w_down):
    f32 = np.float32
    xp = np.asarray(x_prompt, f32)[0]
    xs = np.asarray(x_sample, f32).reshape(NCORES, TS, D)
    skk = np.asarray(state_k, f32)[0].reshape(128, 2048, AW)
    svv = np.asarray(state_v, f32)[0].reshape(128, 2048, AW)
    scc = np.asarray(state_conv, f32)[0].reshape(128 * 2, AW)
    Wp, keep, Wnear, Wfar, Wnew = _tables()
    if "nc" not in _CACHE:
        _CACHE["nc"] = build_program(keep, _STOP)
    nc = _CACHE["nc"]

    w_in0 = np.ascontiguousarray(np.asarray(w_in, f32)[0])
    w_out0 = np.ascontiguousarray(np.asarray(w_out, f32)[0])
    w_up0 = np.ascontiguousarray(np.asarray(w_up, f32)[0])
    w_down0 = np.ascontiguousarray(np.asarray(w_down, f32)[0])
    g1b = np.ascontiguousarray(np.broadcast_to(np.asarray(norm1_g, f32)[0][None, :], (128, D)))
    g2b = np.ascontiguousarray(np.broadcast_to(np.asarray(norm2_g, f32)[0][None, :], (128, D)))
    qgb = np.ascontiguousarray(np.broadcast_to(np.tile(np.asarray(q_norm_g, f32)[0], NH)[None, :], (128, AW)))
    kgb = np.ascontiguousarray(np.broadcast_to(np.tile(np.asarray(k_norm_g, f32)[0], NH)[None, :], (128, AW)))
    cwT = np.ascontiguousarray(np.asarray(conv_w, f32)[0].reshape(3, 8, 128).transpose(2, 1, 0))
    ident = np.eye(128, dtype=f32)

    in_maps = []
    for c in range(NCORES):
        c0 = c * TP
        xh = np.zeros((NHALO, D), f32)
        lo = max(0, c0 - NHALO)
        if c0 > 0:
            xh[NHALO - (c0 - lo):] = xp[lo:c0]
        x2 = np.zeros((128, D), f32)
        x2[0:2] = xh[NHALO - 2:NHALO]
        xo = np.concatenate([xp[c0:c0 + TP], xs[c]], axis=0)
        tok = c0 - NHALO + np.arange(24 * 128)
        valid = np.ascontiguousarray((tok >= 0).astype(f32).reshape(24, 128).T)
        in_maps.append({
            "xh": xh, "xo": np.ascontiguousarray(xo), "x2": x2,
            "sk": skk[c * NB:(c + 1) * NB], "sv": svv[c * NB:(c + 1) * NB],
            "sc": scc[c * 2 * NB:(c + 1) * 2 * NB],
            "w_in": w_in0, "w_out": w_out0, "w_up": w_up0, "w_down": w_down0,
            "g1b": g1b, "g2b": g2b, "qgb": qgb, "kgb": kgb, "cwT": cwT, "valid": valid,
            "ident": ident, "Wp": Wp, "Wnear": Wnear, "Wfar": Wfar, "Wnew": Wnew,
        })
    res = run_bass_kernel_spmd(nc, in_maps, core_ids=list(range(NCORES)))
    R = res.results
    y = np.stack([np.asarray(r["y"], f32) for r in R])
    ko = np.stack([np.asarray(r["ko"], f32) for r in R])
    vo = np.stack([np.asarray(r["vo"], f32) for r in R])
    uo = np.stack([np.asarray(r["uo"], f32) for r in R])
    y_prompt = y[:, :TP].reshape(1, 8192, D)
    y_sample = y[:, TP:].reshape(128, 8, D)
    nkp = ko[6:8, :TP].reshape(1, 1, 2048, NH, HD)
    nvp = vo[6:8, :TP].reshape(1, 1, 2048, NH, HD)
    ncp = uo[7, 0:2].reshape(1, 1, 2, AW)
    nks = ko[:, TP:].reshape(1, 128, 8, NH, HD)
    nvs = vo[:, TP:].reshape(1, 128, 8, NH, HD)
    ncs = uo[:, 2:34].reshape(1, 128, 2, AW)
    return (np.ascontiguousarray(y_prompt), np.ascontiguousarray(y_sample), np.ascontiguousarray(nkp),
            np.ascontiguousarray(nvp), np.ascontiguousarray(ncp), np.ascontiguousarray(nks),
            np.ascontiguousarray(nvs), np.ascontiguousarray(ncs))
```
